# Optimizing a Trainium2 kernel written in Bass

```python
import jax, jax.numpy as jnp
from jax import lax
import numpy as np

D_MODEL = 1024
BATCH = 32
SEQ = 256
DEPTH = 4
DEC_BATCH = 2
DEC_SEQ = 2048
PAST_LEN = 256

GRID_W = 64
N_HEADS = 8
HEAD_DIM = 64
ATTN_WIDTH = N_HEADS * HEAD_DIM
CONV_WIDTH = D_MODEL // 2
CONV_K = 3
D_FF = 4 * D_MODEL
WIN_R = 8
WIN_C = 16
Q_BLOCK = 128
IN_COLS = 3 * ATTN_WIDTH + 3 * CONV_WIDTH + 2 * D_MODEL
SPLIT_POINTS = (ATTN_WIDTH, 2 * ATTN_WIDTH, 3 * ATTN_WIDTH,
                3 * ATTN_WIDTH + CONV_WIDTH, 3 * ATTN_WIDTH + 2 * CONV_WIDTH,
                3 * ATTN_WIDTH + 3 * CONV_WIDTH, 3 * ATTN_WIDTH + 3 * CONV_WIDTH + D_MODEL)
SCALE = HEAD_DIM ** -0.5
ALPHA = (2.0 * DEPTH) ** 0.25
BETA = (8.0 * DEPTH) ** -0.25
LN_EPS = 1e-5

kernel_name = "hybrid_dit_natten_shortconv_step"


def _layer_norm(x, g, b):
    xf = x.astype(jnp.float32)
    mu = jnp.mean(xf, axis=-1, keepdims=True)
    var = jnp.mean(jnp.square(xf - mu), axis=-1, keepdims=True)
    return ((xf - mu) * lax.rsqrt(var + LN_EPS)).astype(x.dtype) * g + b


def _heads(t):
    b, l, _ = t.shape
    return t.reshape(b, l, N_HEADS, HEAD_DIM).transpose(0, 2, 1, 3)


def _merge_heads(t):
    b, h, l, d = t.shape
    return t.transpose(0, 2, 1, 3).reshape(b, l, h * d)


def _short_conv(z, w, bias):
    zp = jnp.pad(z, ((0, 0), (1, 1), (0, 0)))
    return zp[:, :-2] * w[0] + zp[:, 1:-1] * w[1] + zp[:, 2:] * w[2] + bias


def _context_attention(q, k, v):
    b, h, l, d = q.shape
    nb = l // Q_BLOCK
    qb = q.reshape(b, h, nb, Q_BLOCK, d).transpose(2, 0, 1, 3, 4)

    def blk(qi):
        s = jnp.einsum('bhqd,bhkd->bhqk', qi, k).astype(jnp.float32) * SCALE
        p = jax.nn.softmax(s, axis=-1).astype(v.dtype)
        return jnp.einsum('bhqk,bhkd->bhqd', p, v)

    o = lax.map(blk, qb)
    return o.transpose(1, 2, 0, 3, 4).reshape(b, h, l, d)


def _latent_neighbourhood_attention(q, k, v, k_ctx, v_ctx, rpb):
    b, h, t, d = q.shape
    rows = t // GRID_W
    kh = min(WIN_R, rows)
    qg = q.reshape(b, h, rows, GRID_W, d)
    kg = k.reshape(b, h, rows, GRID_W, d)
    vg = v.reshape(b, h, rows, GRID_W, d)
    cols = jnp.arange(GRID_W)
    col_start = jnp.clip(cols - WIN_C // 2, 0, GRID_W - WIN_C)
    col_mask = (cols[None, :] >= col_start[:, None]) & (cols[None, :] < col_start[:, None] + WIN_C)
    col_idx = jnp.clip(cols[None, :] - cols[:, None] + WIN_C - 1, 0, 2 * WIN_C - 2)
    rpb_cols = rpb[:, :, col_idx]
    n_band = kh * GRID_W

    def row_fn(args):
        q_r, r = args
        sr = jnp.clip(r - kh // 2, 0, rows - kh)
        k_b = lax.dynamic_slice_in_dim(kg, sr, kh, axis=2)
        v_b = lax.dynamic_slice_in_dim(vg, sr, kh, axis=2)
        row_idx = sr + jnp.arange(kh) - r + WIN_R - 1
        bias = rpb_cols[:, row_idx].transpose(0, 2, 1, 3)
        s_band = jnp.einsum('bhqd,bhikd->bhqik', q_r, k_b).astype(jnp.float32) * SCALE
        s_band = s_band + bias[None].astype(jnp.float32)
        s_band = jnp.where(col_mask[None, None, :, None, :], s_band, -jnp.inf)
        s_ctx = jnp.einsum('bhqd,bhld->bhql', q_r, k_ctx).astype(jnp.float32) * SCALE
        s = jnp.concatenate([s_band.reshape(b, h, GRID_W, n_band), s_ctx], axis=-1)
        p = jax.nn.softmax(s, axis=-1).astype(v.dtype)
        p_band = p[..., :n_band].reshape(b, h, GRID_W, kh, GRID_W)
        p_ctx = p[..., n_band:]
        return (jnp.einsum('bhqik,bhikd->bhqd', p_band, v_b)
                + jnp.einsum('bhql,bhld->bhqd', p_ctx, v_ctx))

    o = lax.map(row_fn, (qg.transpose(2, 0, 1, 3, 4), jnp.arange(rows)))
    return o.transpose(1, 2, 0, 3, 4).reshape(b, h, t, d)


def _layer(x, cond, attend, w_mod, b_mod, w_in, conv_w, conv_b, w_attn_proj, w_conv_proj, w_o,
           ln1_g, ln1_b, w1, b1, w2, b2, ln2_g, ln2_b):
    mod = (jax.nn.silu(cond) @ w_mod + b_mod)[:, None, :]
    sh1, sc1, g1, sh2, sc2, g2 = jnp.split(mod, 6, axis=-1)
    h = x * (1 + sc1) + sh1
    q, k, v, u, bg, cg, ga, gc = jnp.split(h @ w_in, SPLIT_POINTS, axis=-1)
    k = _heads(k)
    v = _heads(v)
    attn = _merge_heads(attend(_heads(q), k, v)) @ w_attn_proj
    conv = (bg * _short_conv(cg * u, conv_w, conv_b)) @ w_conv_proj
    mix = (jax.nn.sigmoid(ga) * attn + jax.nn.sigmoid(gc) * conv) @ w_o
    x = _layer_norm(ALPHA * x + g1 * mix, ln1_g, ln1_b)
    h = x * (1 + sc2) + sh2
    f = jnp.square(jax.nn.relu(h @ w1 + b1)) @ w2 + b2
    x = _layer_norm(ALPHA * x + g2 * f, ln2_g, ln2_b)
    return x, k, v


def setup_inputs(seed: int = 0) -> dict:
    key = jax.random.key(seed)
    ks = jax.random.split(key, 32)
    n = jax.random.normal
    f32 = jnp.float32
    d = D_MODEL
    return {
        'x_prompt': n(ks[0], (BATCH, SEQ, d), f32),
        'x_sample': n(ks[1], (DEC_BATCH, DEC_SEQ, d), f32),
        'cache_k': n(ks[2], (DEC_BATCH, DEPTH, N_HEADS, PAST_LEN, HEAD_DIM), f32),
        'cache_v': n(ks[3], (DEC_BATCH, DEPTH, N_HEADS, PAST_LEN, HEAD_DIM), f32),
        'c': n(ks[4], (DEC_BATCH, d), f32),
        'c_ctx': n(ks[5], (d,), f32),
        'w_mod': n(ks[6], (DEPTH, d, 6 * d), f32) * (0.5 * d ** -0.5),
        'b_mod': n(ks[7], (DEPTH, 6 * d), f32) * 0.01,
        'w_in': n(ks[8], (DEPTH, d, IN_COLS), f32) * d ** -0.5,
        'rpb': n(ks[9], (DEPTH, N_HEADS, 2 * WIN_R - 1, 2 * WIN_C - 1), f32) * 0.5,
        'conv_w': n(ks[10], (DEPTH, CONV_K, CONV_WIDTH), f32) * CONV_K ** -0.5,
        'conv_b': n(ks[11], (DEPTH, CONV_WIDTH), f32) * 0.01,
        'w_attn_proj': n(ks[12], (DEPTH, ATTN_WIDTH, d), f32) * ATTN_WIDTH ** -0.5,
        'w_conv_proj': n(ks[13], (DEPTH, CONV_WIDTH, d), f32) * CONV_WIDTH ** -0.5,
        'w_o': n(ks[14], (DEPTH, d, d), f32) * (BETA * d ** -0.5),
        'ln1_g': 1.0 + 0.01 * n(ks[15], (DEPTH, d), f32),
        'ln1_b': 0.01 * n(ks[16], (DEPTH, d), f32),
        'w1': n(ks[17], (DEPTH, d, D_FF), f32) * d ** -0.5,
        'b1': 0.01 * n(ks[18], (DEPTH, D_FF), f32),
        'w2': n(ks[19], (DEPTH, D_FF, d), f32) * (BETA * D_FF ** -0.5),
        'b2': 0.01 * n(ks[20], (DEPTH, d), f32),
        'ln2_g': 1.0 + 0.01 * n(ks[21], (DEPTH, d), f32),
        'ln2_b': 0.01 * n(ks[22], (DEPTH, d), f32),
    }


def reference(x_prompt, x_sample, cache_k, cache_v, c, c_ctx, w_mod, b_mod, w_in, rpb, conv_w, conv_b,
              w_attn_proj, w_conv_proj, w_o, ln1_g, ln1_b, w1, b1, w2, b2, ln2_g, ln2_b):
    xp = x_prompt
    cond_ctx = c_ctx[None, :]
    ks_list = []
    vs_list = []
    for l in range(DEPTH):
        xp, k_l, v_l = _layer(xp, cond_ctx, _context_attention, w_mod[l], b_mod[l], w_in[l], conv_w[l],
                              conv_b[l], w_attn_proj[l], w_conv_proj[l], w_o[l], ln1_g[l], ln1_b[l],
                              w1[l], b1[l], w2[l], b2[l], ln2_g[l], ln2_b[l])
        ks_list.append(k_l)
        vs_list.append(v_l)
    new_k = jnp.stack(ks_list, axis=1)
    new_v = jnp.stack(vs_list, axis=1)

    xs = x_sample
    for l in range(DEPTH):
        k_ctx = cache_k[:, l]
        v_ctx = cache_v[:, l]
        rpb_l = rpb[l]
        attend = lambda q, k, v, k_ctx=k_ctx, v_ctx=v_ctx, rpb_l=rpb_l: _latent_neighbourhood_attention(
            q, k, v, k_ctx, v_ctx, rpb_l)
        xs, _, _ = _layer(xs, c, attend, w_mod[l], b_mod[l], w_in[l], conv_w[l], conv_b[l],
                          w_attn_proj[l], w_conv_proj[l], w_o[l], ln1_g[l], ln1_b[l],
                          w1[l], b1[l], w2[l], b2[l], ln2_g[l], ln2_b[l])
    return (xp, xs, new_k, new_v)
```

```python
import numpy as np
from contextlib import ExitStack
import concourse.bass as bass
import concourse.mybir as mybir
from concourse.bass_utils import run_bass_kernel_spmd

F32 = mybir.dt.float32
BF16 = mybir.dt.bfloat16
AF = mybir.ActivationFunctionType
ALU = mybir.AluOpType

D = 1024
DEPTH = 4
NTOK = 2048
NT = 512
NTILES = 4
H = 8
HD = 64
DFF = 4096
INC = 5120
SCALE = HD ** -0.5
ALPHA = (2.0 * DEPTH) ** 0.25
LN_EPS = 1e-5
EPS_P = LN_EPS / (ALPHA * ALPHA)
NEG = -30000.0
NCORES = 8
JR = {0: (0, 2), 1: (0, 4), 2: (0, 8), 3: (0, 8), 4: (0, 8), 5: (0, 8), 6: (5, 8), 7: (7, 8)}
TMIN = 4
NTM = 14
MW = NTM * 64
ENGS = ("pe", "act", "dve", "pool", "sp")
DRY = [False]
LIMIT = [10 ** 9]
DBG = [99]
HENG = ["dve"]
LNENG = ["dve"]
EWPOOL = [()]
SPILL = [True]


class Trk:
    def __init__(self, target, eng, sem):
        self.target = target
        self.eng = eng
        self.cnt = {e: 0 for e in ENGS}
        self.known = {e: {} for e in ENGS}
        self.last_w = {}
        self.readers = {}
        self.dma_cnt = {}
        self.sem = sem
        self.rec = {e: [] for e in ENGS} if target is None else None
        self.lab = ""
        self.pelabs = []

    def _need(self, reads, writes):
        need = {}

        def add(k, v):
            if need.get(k, 0) < v:
                need[k] = v
        for r in reads:
            lw = self.last_w.get(r)
            if lw:
                add(*lw)
            if r[0] in ("PSG", "PSA"):
                for k, v in self.readers.get(r, {}).items():
                    add(k, v)
        for w in writes:
            lw = self.last_w.get(w)
            if lw:
                add(*lw)
            for k, v in self.readers.get(w, {}).items():
                add(k, v)
        return need

    def _waits(self, e, need):
        for k, v in need.items():
            if k == e and e == "pe":
                continue
            if self.known[e].get(k, 0) >= v:
                continue
            self.known[e][k] = v
            if self.rec is not None:
                self.rec[e].append(("wait", k, v))
            if e == self.target:
                self.eng.wait_ge(self.sem[k], v)

    def op(self, e, fn, reads=(), writes=()):
        self._waits(e, self._need(reads, writes))
        self.cnt[e] += 1
        idx = self.cnt[e]
        if self.rec is not None:
            self.rec[e].append(("inc", e, 1))
            if e == "pe":
                self.pelabs.append(self.lab)
        if e == self.target:
            fn(self.eng).then_inc(self.sem[e], 1)
        for r in reads:
            self.readers.setdefault(r, {})[e] = idx
        for w in writes:
            self.last_w[w] = (e, idx)
            self.readers[w] = {}

    def dma(self, q, fn, semkey, reads=(), writes=()):
        self._waits(q, self._need(reads, writes))
        self.dma_cnt[semkey] = self.dma_cnt.get(semkey, 0) + 16
        val = self.dma_cnt[semkey]
        if self.rec is not None:
            self.rec[q].append(("inc", semkey, 16))
        if q == self.target:
            fn(self.eng).then_inc(self.sem[semkey], 16)
        for r in reads:
            self.readers.setdefault(r, {})[semkey] = val
        for w in writes:
            self.last_w[w] = (semkey, val)
            self.readers[w] = {}

    def fix_batch(self, semkey):
        val = self.dma_cnt[semkey]
        for k, (sk, v) in list(self.last_w.items()):
            if sk == semkey:
                self.last_w[k] = (sk, val)

    def barrier(self):
        tot = {e: self.cnt[e] for e in ("pe", "act", "dve", "pool") if self.cnt[e] > 0}
        for sk, v in self.dma_cnt.items():
            tot[sk] = v
        for e in ("pe", "act", "dve", "pool"):
            need = {k: v for k, v in tot.items() if k != e}
            self._waits(e, need)


def _jr_per_tile():
    dummy = np.zeros((DEPTH, H, 15, 31), np.float32)
    ews = [_core_tables(r, dummy)[0].reshape(128, 4, 8, 8) for r in ("sample", "prompt")]
    out = []
    for tt in range(NTILES):
        row = {}
        for c in range(8):
            v = np.zeros(8, bool)
            for ew in ews:
                v |= (ew[:, tt, c, :] != 0).any(axis=0)
            js = np.nonzero(v)[0]
            row[c] = (int(js[0]), int(js[-1]) + 1) if len(js) else None
        out.append(row)
    return out


def build_program():
    global JRT
    JRT = _jr_per_tile()
    nc = bass.Bass("TRN2", target_bir_lowering=False)
    SEM = {}

    def din(name, shape, dt=F32):
        return nc.dram_tensor(name, list(shape), dt, kind="ExternalInput").ap()

    def dout(name, shape):
        return nc.dram_tensor(name, list(shape), F32, kind="ExternalOutput").ap()

    xT_d = din("xT", [D, NTOK])
    cond_d = din("cond_pc", [128, 8])
    kctx_d = din("kctxT", [DEPTH, 128, 4, 256])
    vctx_d = din("vctx", [DEPTH, 128, 2, 768])
    ctxb_d = din("ctxb", [128, 1])
    ew_d = din("ew", [128, 256])
    mtab_d = din("mtab", [DEPTH, H, 128, MW])
    nf_d = din("nfm", [1, NTOK + 2])
    wmod_d = din("w_mod", [DEPTH, D, 6 * D])
    bmod_d = din("bmod_pc", [128, DEPTH * 48])
    win_d = din("w_in", [DEPTH, D, INC])
    cw_d = din("convw_pc", [128, DEPTH * 4 * 3])
    cb_d = din("convb_pc", [128, DEPTH * 4])
    wap_d = din("w_attn_proj", [DEPTH, 512, D])
    wcp_d = din("w_conv_proj", [DEPTH, 512, D])
    wo_d = din("w_o", [DEPTH, D, D])
    l1g_d = din("ln1g_pc", [128, DEPTH * 8])
    l1b_d = din("ln1b_pc", [128, DEPTH * 8])
    w1_d = din("w1", [DEPTH, D, DFF])
    b1_d = din("b1_pc", [128, DEPTH * 32])
    w2_d = din("w2", [DEPTH, DFF, D])
    b2_d = din("b2_pc", [128, DEPTH * 8])
    l2g_d = din("ln2g_pc", [128, DEPTH * 8])
    l2b_d = din("ln2b_pc", [128, DEPTH * 8])

    yT_o = dout("yT", [D, NTOK])
    kT_o = dout("kT_out", [DEPTH, 512, NTOK])
    v_o = dout("v_out", [DEPTH, NTOK, 512])

    SCR = nc.dram_tensor("scr_bf16", [12, 128, 4096], BF16).ap()
    es = ExitStack()

    def sb(name, shape, dt):
        return es.enter_context(nc.sbuf_tensor(name, list(shape), dt))

    X = sb("X", [128, 8, NTOK], F32)
    BIG = sb("BIG", [128, 20480], BF16)
    Z = sb("Z", [128, 4, NTOK + 2], BF16)
    HM = sb("HM", [128, 8, NT], BF16)
    G1 = sb("G1", [128, 8, NT], BF16)
    QA = sb("QA", [128, 8, NT], BF16)
    S = [sb(f"S{i}", [128, NT], F32) for i in range(2)]
    PT = [sb(f"PT{i}", [128, NT], BF16) for i in range(5)]
    MS = [sb(f"MS{i}", [128, MW], F32) for i in range(2)]
    EM = [sb(f"EM{i}", [128, MW], BF16) for i in range(2)]
    EWB = sb("EWB", [128, 256], BF16)
    RING = [sb(f"RG{i}", [128, 4096], BF16) for i in range(3)]
    L = [sb(f"L{i}", [128, NT], F32) for i in range(3)]
    NF = sb("NF", [128, NTOK + 2], BF16)
    KCX = sb("KCX", [128, 4, 256], BF16)
    VCX = sb("VCX", [128, 2, 768], BF16)
    MODV = sb("MODV", [128, 2 * 48], F32)
    BMOD = sb("BMOD", [128, DEPTH * 48], F32)
    L1G = sb("L1G", [128, DEPTH * 8], F32)
    L1B = sb("L1B", [128, DEPTH * 8], F32)
    L2G = sb("L2G", [128, DEPTH * 8], F32)
    L2B = sb("L2B", [128, DEPTH * 8], F32)
    B1 = sb("B1", [128, DEPTH * 32], F32)
    B2 = sb("B2", [128, DEPTH * 8], F32)
    GB2 = sb("GB2", [128, 8], F32)
    CW = sb("CW", [128, DEPTH * 12], F32)
    CB = sb("CB", [128, DEPTH * 4], F32)
    EW = sb("EW", [128, 256], F32)
    ONES = sb("ONES", [128, 128], BF16)
    CONDT = sb("CONDT", [128, 8], F32)
    SCB = sb("SCB", [128, 8], BF16)
    CTXB = sb("CTXB", [128, 1], F32)
    EPSC = sb("EPSC", [128, 1], F32)
    ONEF = sb("ONEF", [128, 1], F32)

    PSG = [es.enter_context(nc.psum_tensor(f"PSG{i}", [128, NT], F32)) for i in range(5)]
    PSA = [es.enter_context(nc.psum_tensor(f"PSA{i}", [128, NT], F32)) for i in range(2)]
    PSM = es.enter_context(nc.psum_tensor("PSM", [128, NT], F32))

    semnames = ["pe", "act", "dve", "pool", "ring0", "ring1", "ring2", "xin", "cst", "m0", "m1",
                "ctxk", "ctxv", "nfs", "sp0", "sp1", "sp2", "oS0", "oS1", "oS2", "oS3", "oS4", "yout"]
    for n in semnames:
        SEM[n] = es.enter_context(nc.semaphore("s_" + n))

    KT = BIG[:, 0:8192].rearrange("p (h t) -> p h t", h=4)
    V = BIG[:, 8192:20480].rearrange("p (c h f) -> p c h f", c=16, h=4)
    H2 = BIG[:, 0:16384].rearrange("p (c t) -> p c t", c=8)
    VCX4 = VCX[:, :, :].rearrange("p c (h f) -> p c h f", h=4)

    def schedule(T):
      st = {"gen": 0, "acc": 0, "ring": 0, "S": 0, "PT": 0, "stage": 0, "stg": 0}

      def go():
          st["stage"] += 1
          return st["stage"] <= LIMIT[0]

      def psg():
          i = st["gen"] % 5
          st["gen"] += 1
          return PSG[i], ("PSG", i)

      def psa():
          i = st["acc"] % 2
          st["acc"] += 1
          return PSA[i], ("PSA", i)

      def nxtS():
          i = st["S"] % 2
          st["S"] += 1
          return S[i], ("S", i), i

      STG = [(S[0], ("S", 0)), (S[1], ("S", 1)), (L[0], ("L", 0)), (L[1], ("L", 1)), (L[2], ("L", 2))]

      def nxtStg():
          i = st["stg"] % 5
          st["stg"] += 1
          return STG[i][0], STG[i][1], i

      def nxtPT():
          i = st["PT"] % 5
          st["PT"] += 1
          return PT[i], ("PT", i)

      def wload(src_ap, kk, ncols):
          i = st["ring"] % 3
          st["ring"] += 1
          view = RING[i][:, 0:kk * ncols].rearrange("p (k n) -> p k n", k=kk)
          T.dma("pool", lambda q, view=view, src_ap=src_ap: q.dma_start(out=view, in_=src_ap),
                f"ring{i}", writes=[("RG", i)])
          return view, ("RG", i)

      def wsrc(wd, l, r0, nrows, c0, ncols):
          return wd[l, r0:r0 + nrows, c0:c0 + ncols].rearrange("(k p) n -> p k n", p=128)

      def wloadB(gi, src_ap, kk, ncols, tt):
          if not SPILL[0]:
              return wload(src_ap, kk, ncols)
          n = kk * ncols
          if tt == 0:
              view, key = wload(src_ap, kk, ncols)
              i = key[1]
              T.dma("sp", lambda q, i=i, gi=gi, n=n: q.dma_start(out=SCR[gi, :, 0:n], in_=RING[i][:, 0:n]),
                    f"sp{i}", reads=[key], writes=[("SCR", gi)])
              return view, key
          i = st["ring"] % 3
          st["ring"] += 1
          view = RING[i][:, 0:n].rearrange("p (k n) -> p k n", k=kk)
          T.dma("pool", lambda q, i=i, gi=gi, n=n: q.dma_start(out=RING[i][:, 0:n], in_=SCR[gi, :, 0:n]),
                f"ring{i}", reads=[("SCR", gi)], writes=[("RG", i)])
          return view, ("RG", i)

      for c in range(8):
          T.dma("sp", lambda q, c=c: q.dma_start(out=X[:, c, :], in_=xT_d[c * 128:(c + 1) * 128, :]),
                "xin", writes=[("X", t, c) for t in range(NTILES)])
      T.fix_batch("xin")
      smalls = [(CONDT, cond_d), (BMOD, bmod_d), (L1G, l1g_d), (L1B, l1b_d), (L2G, l2g_d), (L2B, l2b_d),
                (B1, b1_d), (B2, b2_d), (CW, cw_d), (CB, cb_d), (EW, ew_d), (CTXB, ctxb_d)]
      for t_sb, t_d in smalls:
          T.dma("sp", lambda q, t_sb=t_sb, t_d=t_d: q.dma_start(out=t_sb[:, :], in_=t_d), "cst",
                writes=[("C", t_sb.name)])
      T.dma("pool", lambda q: q.dma_start(out=NF[:, :], in_=nf_d.partition_broadcast(128)), "nfs",
            writes=[("C", "NF")])
      T.fix_batch("cst")
      CK = lambda t: ("C", t.name)

      T.op("dve", lambda e: e.memset(ONES[:, :], 1.0), writes=[("C", "ONES")])
      T.op("dve", lambda e: e.memset(EPSC[:, :], EPS_P), writes=[("C", "EPSC")])
      T.op("dve", lambda e: e.memset(ONEF[:, :], 1.0), writes=[("C", "ONEF")])
      T.op("dve", lambda e: e.memset(Z[:, :, :], 0.0), writes=[("Z", t) for t in range(NTILES)] + [("Zpad",)])
      T.op("dve", lambda e: e.memset(BIG[:, 8192:20480], 1.0), writes=[("V", i) for i in range(16)])
      T.op("dve", lambda e: e.memset(VCX[:, :, :], 1.0), writes=[("VCX",)])
      T.op("dve", lambda e: e.tensor_copy(out=EWB[:, :], in_=EW[:, :]), reads=[CK(EW)], writes=[("C", "EWB")])
      T.op("act", lambda e: e.activation(out=SCB[:, :], in_=CONDT[:, :], func=AF.Silu),
           reads=[CK(CONDT)], writes=[("C", "SCB")])

      pcol, pck = PSM, ("PSA", 9)

      def mod_load(l, g):
          return wload(wsrc(wmod_d, l, 0, D, g * 512, 512), 8, 512)

      def mod_group(l, g, pre=None):
          lab0 = T.lab
          T.lab = f"L{l}.mod"
          wv, wk = pre if pre is not None else mod_load(l, g)
          prow, prk = psg()
          for kc in range(8):
              T.op("pe", lambda e, wv=wv, kc=kc, prow=prow: e.matmul(
                  prow[0:1, :], lhsT=SCB[:, kc:kc + 1], rhs=wv[:, kc, :], start=(kc == 0), stop=(kc == 7)),
                  reads=[wk, ("C", "SCB")], writes=[prk])
          rhi, rhk = nxtPT()
          rlo, rlk = nxtPT()
          T.op("act", lambda e, rhi=rhi, prow=prow: e.activation(out=rhi[0:1, :], in_=prow[0:1, :],
                                                                   func=AF.Identity),
               reads=[prk], writes=[rhk])
          T.op("dve", lambda e, rhi=rhi, rlo=rlo, prow=prow: e.tensor_tensor(
              out=rlo[0:1, :], in0=prow[0:1, :], in1=rhi[0:1, :], op=ALU.subtract),
              reads=[prk, rhk], writes=[rlk])
          for fl in range(4):
              col = g * 4 + fl
              T.op("pe", lambda e, rhi=rhi, fl=fl, col=col: e.matmul(
                  pcol[:, col:col + 1], lhsT=rhi[0:1, fl * 128:(fl + 1) * 128], rhs=ONES[0:1, 0:1],
                  start=True, stop=False), reads=[rhk, ("C", "ONES")], writes=[pck])
              T.op("pe", lambda e, rlo=rlo, fl=fl, col=col: e.matmul(
                  pcol[:, col:col + 1], lhsT=rlo[0:1, fl * 128:(fl + 1) * 128], rhs=ONES[0:1, 0:1],
                  start=False, stop=True), reads=[rlk, ("C", "ONES")], writes=[pck])
          T.lab = lab0

      def mod_finish(l):
          mo = (l % 2) * 48
          mk = ("MODV", l % 2)
          T.op("dve", lambda e: e.tensor_tensor(out=MODV[:, mo:mo + 48], in0=pcol[:, 0:48],
                                                in1=BMOD[:, l * 48:(l + 1) * 48], op=ALU.add),
               reads=[pck, CK(BMOD)], writes=[mk])
          for a_ in (8, 32):
              T.op("dve", lambda e, a_=a_: e.tensor_scalar_add(out=MODV[:, mo + a_:mo + a_ + 8],
                                                              in0=MODV[:, mo + a_:mo + a_ + 8], scalar1=1.0),
                   reads=[mk], writes=[mk])
          for a_ in (16, 40):
              T.op("dve", lambda e, a_=a_: e.tensor_scalar_mul(out=MODV[:, mo + a_:mo + a_ + 8],
                                                              in0=MODV[:, mo + a_:mo + a_ + 8], scalar1=1.0 / ALPHA),
                   reads=[mk], writes=[mk])

      def emit_mod(l):
          if not go():
              return
          for g in range(12):
              mod_group(l, g)
          mod_finish(l)

      def emit_h(l, tt, which, dst_fn, dkeys):
          mo = (l % 2) * 48 + which * 24
          for c in range(8):
              T.op(HENG[0], lambda e, c=c: e.tensor_scalar(
                  out=dst_fn(c), in0=X[:, c, tt * NT:(tt + 1) * NT], scalar1=MODV[:, mo + 8 + c:mo + 8 + c + 1],
                  scalar2=MODV[:, mo + c:mo + c + 1], op0=ALU.mult, op1=ALU.add),
                  reads=[("X", tt, c), ("MODV", l % 2)],
                  writes=(dkeys(c) if isinstance(dkeys(c), list) else [dkeys(c)]))

      def proj_fm(wv, wk, ncol0, rhs_fn, rkeys, nk=8):
          ps, pk = psg()
          for kc in range(nk):
              rk_ = rkeys(kc) if callable(rkeys) else rkeys
              T.op("pe", lambda e, kc=kc: e.matmul(ps[:, :], lhsT=wv[:, kc, ncol0:ncol0 + 128], rhs=rhs_fn(kc),
                                                   start=(kc == 0), stop=(kc == nk - 1)),
                   reads=[wk] + rk_, writes=[pk])
          return ps, pk

      def out_dma(src_ap, skey, si, dst_ap):
          T.dma("sp", lambda q: q.dma_start(out=dst_ap, in_=src_ap), f"oS{si}", reads=[skey])

      QK = lambda i: [("QA", i, 0), ("QA", i, 1)]
      HMk = lambda c: ("HM", c)
      HMall = lambda kc: [("HM", kc)]

      def passA2(l, tiles):
          if not go():
              return
          T.lab = "A.kvuc"
          hb = {}
          for i, tt in enumerate(tiles):
              if i == 0:
                  emit_h(l, tt, 0, lambda c: HM[:, c, :], HMk)
                  hb[tt] = (HM, lambda kc: [("HM", kc)])
              else:
                  emit_h(l, tt, 0, lambda c: QA[:, c, :], QK)
                  hb[tt] = (QA, lambda kc: [("QA", kc, 0), ("QA", kc, 1)])
          wv, wk = wload(wsrc(win_d, l, 0, D, 512, 512), 8, 512)
          for tt in tiles:
              HB, HK = hb[tt]
              tsl = slice(tt * NT, (tt + 1) * NT)
              for hp in range(4):
                  ps, pk = proj_fm(wv, wk, hp * 128, lambda kc, HB=HB: HB[:, kc, :], HK)
                  sbuf, sk, si = nxtStg()
                  T.op("act", lambda e, ps=ps, sbuf=sbuf: e.activation(out=sbuf[:, :], in_=ps[:, :],
                                                                       func=AF.Identity), reads=[pk], writes=[sk])
                  T.op("dve", lambda e, hp=hp, sbuf=sbuf, tsl=tsl: e.tensor_copy(out=KT[:, hp, tsl], in_=sbuf[:, :]),
                       reads=[sk], writes=[("KT", tt, hp)])
                  out_dma(sbuf[:, :], sk, si, kT_o[l, hp * 128:(hp + 1) * 128, tsl])
          wv, wk = wload(wsrc(win_d, l, 0, D, 1024, 512), 8, 512)
          for tt in tiles:
              HB, HK = hb[tt]
              for tb in range(4):
                  ps, pk = psg()
                  for kc in range(8):
                      T.op("pe", lambda e, kc=kc, tb=tb, ps=ps, wv=wv, HB=HB: e.matmul(
                          ps[:, :], lhsT=HB[:, kc, tb * 128:(tb + 1) * 128], rhs=wv[:, kc, :],
                          start=(kc == 0), stop=(kc == 7)), reads=[wk] + HK(kc), writes=[pk])
                  ci = tt * 4 + tb
                  sbuf, sk, si = nxtStg()
                  T.op("act", lambda e, ps=ps, sbuf=sbuf: e.activation(out=sbuf[:, :], in_=ps[:, :],
                                                                       func=AF.Identity), reads=[pk], writes=[sk])
                  T.op("dve", lambda e, ci=ci, sbuf=sbuf: e.tensor_copy(
                      out=V[:, ci, :, :].rearrange("p h (s f) -> p h s f", s=3)[:, :, 0:3:2, :],
                      in_=sbuf[:, :].rearrange("p (h s f) -> p h s f", h=4, s=2)),
                      reads=[sk], writes=[("V", ci)])
                  out_dma(sbuf[:, :], sk, si, v_o[l, tt * NT + tb * 128:tt * NT + (tb + 1) * 128, :])
          wv, wk = wload(wsrc(win_d, l, 0, D, 1536, 512), 8, 512)
          for tt in tiles:
              HB, HK = hb[tt]
              zsl = slice(1 + tt * NT, 1 + (tt + 1) * NT)
              for cc in range(4):
                  ps, pk = proj_fm(wv, wk, cc * 128, lambda kc, HB=HB: HB[:, kc, :], HK)
                  T.op("act", lambda e, cc=cc, ps=ps, zsl=zsl: e.activation(out=Z[:, cc, zsl], in_=ps[:, :],
                                                                            func=AF.Identity),
                       reads=[pk], writes=[("Z", tt, cc)])
          wv, wk = wload(wsrc(win_d, l, 0, D, 2560, 512), 8, 512)
          for tt in tiles:
              HB, HK = hb[tt]
              zsl = slice(1 + tt * NT, 1 + (tt + 1) * NT)
              for cc in range(4):
                  ps, pk = proj_fm(wv, wk, cc * 128, lambda kc, HB=HB: HB[:, kc, :], HK)
                  T.op("dve", lambda e, cc=cc, ps=ps, zsl=zsl: e.tensor_tensor(out=Z[:, cc, zsl], in0=ps[:, :],
                                                                               in1=Z[:, cc, zsl], op=ALU.mult),
                       reads=[pk, ("Z", tt, cc)], writes=[("Z", tt, cc)])

      def attention(l, tt):
          modpre = [None, None]
          JRt = JRT[tt]
          for h in range(H):
              hp, half = h // 2, h % 2
              pb = 64 * half
              em = EM[h % 2]
              emk = ("EM", h % 2)

              def load_em(hh):
                  ms_ = MS[hh % 2]
                  em_ = EM[hh % 2]
                  T.dma("sp", lambda q: q.dma_start(out=ms_[:, :], in_=mtab_d[l, hh]), f"m{hh % 2}",
                        writes=[("MS", hh % 2)])
                  T.op("act", lambda e: e.activation(out=em_[:, :], in_=ms_[:, :], func=AF.Exp),
                       reads=[("MS", hh % 2)], writes=[("EM", hh % 2)])
              if h == 0:
                  load_em(0)
              if h + 1 < H:
                  load_em(h + 1)
              acc, ak = psa()
              order = [c for c in (3, 2, 4, 5, 0, 1, 6, 7) if 0 <= tt * 4 - 2 + c < 16 and JRt[c] is not None]
              assert JRt[3] == (0, 8)
              steps = [("b", c) for c in order]
              steps.insert(1, ("x", 0))
              steps.insert(4, ("x", 1))
              qkey = ("QA", hp, half)
              pend = []

              def do_pv(item, first, last):
                  kind, c, pt, ptk, j0, j1 = item
                  ncols = (j1 - j0) * 64
                  if kind == "b":
                      ci = tt * 4 - 2 + c
                      lhs = V[:, ci, hp, 64 * half:64 * half + 128]
                      vk = ("V", ci)
                  else:
                      lhs = VCX4[:, c, hp, 64 * half:64 * half + 128]
                      vk = ("VCX",)
                  T.op("pe", lambda e: e.matmul(acc[:, j0 * 64:j1 * 64], lhsT=lhs, rhs=pt[:, 0:ncols],
                                                start=first, stop=last),
                       reads=[ptk, vk], writes=[ak])

              n_done = 0
              for si_, (kind, c) in enumerate(steps):
                  ps, pk = psg()
                  pt, ptk = nxtPT()
                  if kind == "b":
                      j0, j1 = JRt[c]
                      ncols = (j1 - j0) * 64
                      g0 = tt * NT - 256 + c * 128
                      T.op("pe", lambda e, ps=ps, g0=g0, j0=j0, j1=j1, ncols=ncols: e.matmul(
                          ps[:, 0:ncols], lhsT=KT[pb:pb + 64, hp, g0:g0 + 128],
                          rhs=QA[pb:pb + 64, hp, j0 * 64:j1 * 64], start=True, stop=True),
                          reads=[("KT", g0 // NT, hp), qkey], writes=[pk])
                      T.op("act", lambda e, ps=ps, pt=pt, ncols=ncols: e.activation(
                          out=pt[:, 0:ncols], in_=ps[:, 0:ncols], func=AF.Exp, scale=SCALE),
                          reads=[pk], writes=[ptk])
                      t0 = j0 - 2 * c + 14
                      m0 = (t0 - TMIN) * 64
                      T.op("dve", lambda e, pt=pt, ncols=ncols, m0=m0: e.tensor_tensor(
                          out=pt[:, 0:ncols], in0=pt[:, 0:ncols], in1=em[:, m0:m0 + ncols], op=ALU.mult),
                          reads=[ptk, emk], writes=[ptk])
                      eb = (tt * 8 + c) * 8
                      nj = j1 - j0
                      T.op("pool" if c in EWPOOL[0] else "dve", lambda e, pt=pt, nj=nj, eb=eb, j0=j0: e.tensor_tensor(
                          out=pt[:, 0:nj * 64].rearrange("p (j q) -> p j q", j=nj),
                          in0=pt[:, 0:nj * 64].rearrange("p (j q) -> p j q", j=nj),
                          in1=EWB[:, eb + j0:eb + j0 + nj].unsqueeze(2).to_broadcast([128, nj, 64]), op=ALU.mult),
                          reads=[ptk, ("C", "EWB")], writes=[ptk])
                  else:
                      j0, j1 = 0, 8
                      T.op("pe", lambda e, ps=ps, c=c: e.matmul(
                          ps[:, :], lhsT=KCX[pb:pb + 64, hp, c * 128:(c + 1) * 128],
                          rhs=QA[pb:pb + 64, hp, :], start=True, stop=True),
                          reads=[("KCX",), qkey], writes=[pk])
                      T.op("act", lambda e, ps=ps, pt=pt: e.activation(
                          out=pt[:, :], in_=ps[:, :], func=AF.Exp, bias=CTXB[:, 0:1], scale=SCALE),
                          reads=[pk, CK(CTXB)], writes=[ptk])
                  pend.append((kind, c, pt, ptk, j0, j1))
                  if len(pend) > 3:
                      do_pv(pend.pop(0), n_done == 0, False)
                      n_done += 1
              while pend:
                  do_pv(pend.pop(0), n_done == 0, len(pend) == 0)
                  n_done += 1
              if l + 1 < DEPTH:
                  if h in (0, 2, 4):
                      modpre[0] = mod_load(l + 1, tt * 3 + h // 2)
                  if h in (2, 4, 6):
                      mod_group(l + 1, tt * 3 + (h - 2) // 2, modpre[1])
                  if h in (0, 2, 4, 6):
                      modpre[1] = modpre[0]
              rd = L[h % 2]
              rk = ("L", h % 2)
              db = 64 - pb
              T.op("act", lambda e, rd=rd: e.activation(out=rd[db:db + 64, :], in_=acc[db:db + 64, :], func=AF.Ln),
                   reads=[ak], writes=[rk])
              T.op("act", lambda e, rd=rd: e.activation(out=rd[db:db + 64, :], in_=rd[db:db + 64, :], func=AF.Exp,
                                                        scale=-1.0), reads=[rk], writes=[rk])
              T.op("dve", lambda e, rd=rd: e.tensor_tensor(out=QA[pb:pb + 64, 4 + hp, :], in0=acc[pb:pb + 64, :],
                                                           in1=rd[db:db + 64, :], op=ALU.mult),
                   reads=[ak, rk], writes=[("QA", 4 + hp, half)])

      def ln_prep(l, tt, YB, ybk, YS, ysk):
          tsl = slice(tt * NT, (tt + 1) * NT)
          xk = [("X", tt, c) for c in range(8)]
          for c in range(8):
              if YS is QA:
                  T.op("dve", lambda e, c=c: e.tensor_copy(out=YB[:, c, :], in_=X[:, c, tsl]),
                       reads=[xk[c]], writes=ybk(c))
              else:
                  T.op("act", lambda e, c=c: e.activation(out=YB[:, c, :], in_=X[:, c, tsl], func=AF.Identity),
                       reads=[xk[c]], writes=ybk(c))
              T.op("act", lambda e, c=c: e.activation(out=YS[:, c, :], in_=X[:, c, tsl], func=AF.Square),
                   reads=[xk[c]], writes=ysk(c))

      def ln_finish(l, tt, G, Bv, YB, ybk, YS, ysk):
          tsl = slice(tt * NT, (tt + 1) * NT)
          xk = [("X", tt, c) for c in range(8)]
          s1, k1 = psg()
          s2, k2 = psg()
          for c in range(8):
              T.op("pe", lambda e, c=c: e.matmul(s1[:, :], lhsT=ONES[:, :], rhs=YB[:, c, :], start=(c == 0),
                                                 stop=(c == 7)), reads=ybk(c) + [("C", "ONES")], writes=[k1])
          for c in range(8):
              T.op("pe", lambda e, c=c: e.matmul(s2[:, :], lhsT=ONES[:, :], rhs=YS[:, c, :], start=(c == 0),
                                                 stop=(c == 7)), reads=ysk(c) + [("C", "ONES")], writes=[k2])
          mean, rstd, nmr = L[0], L[1], L[2]
          T.op("dve", lambda e: e.tensor_scalar_mul(out=mean[:, :], in0=s1[:, :], scalar1=1.0 / D),
               reads=[k1], writes=[("L", 0)])
          T.op("dve", lambda e: e.tensor_tensor(out=nmr[:, :], in0=mean[:, :], in1=mean[:, :], op=ALU.mult),
               reads=[("L", 0)], writes=[("L", 2)])
          T.op("dve", lambda e: e.scalar_tensor_tensor(out=rstd[:, :], in0=s2[:, :], scalar=1.0 / D, in1=nmr[:, :],
                                                       op0=ALU.mult, op1=ALU.subtract),
               reads=[k2, ("L", 2)], writes=[("L", 1)])
          T.op("act", lambda e: e.activation(out=rstd[:, :], in_=rstd[:, :], func=AF.Ln, bias=EPSC[:, 0:1],
                                             scale=1.0), reads=[("L", 1), ("C", "EPSC")], writes=[("L", 1)])
          T.op("act", lambda e: e.activation(out=rstd[:, :], in_=rstd[:, :], func=AF.Exp, scale=-0.5),
               reads=[("L", 1)], writes=[("L", 1)])
          T.op("dve", lambda e: e.scalar_tensor_tensor(out=nmr[:, :], in0=mean[:, :], scalar=-1.0, in1=rstd[:, :],
                                                       op0=ALU.mult, op1=ALU.mult),
               reads=[("L", 0), ("L", 1)], writes=[("L", 2)])
          for c in range(8):
              T.op(LNENG[0], lambda e, c=c: e.tensor_tensor(out=X[:, c, tsl], in0=X[:, c, tsl], in1=rstd[:, :],
                                                         op=ALU.mult), reads=[xk[c], ("L", 1)], writes=[xk[c]])
              T.op(LNENG[0], lambda e, c=c: e.tensor_tensor(out=X[:, c, tsl], in0=X[:, c, tsl], in1=nmr[:, :],
                                                         op=ALU.add), reads=[xk[c], ("L", 2)], writes=[xk[c]])
              T.op(LNENG[0], lambda e, c=c: e.tensor_scalar(out=X[:, c, tsl], in0=X[:, c, tsl],
                                                          scalar1=G[:, l * 8 + c:l * 8 + c + 1],
                                                          scalar2=Bv[:, l * 8 + c:l * 8 + c + 1],
                                                          op0=ALU.mult, op1=ALU.add),
                   reads=[xk[c], CK(G), CK(Bv)], writes=[xk[c]])

      def passB(l, tt, qw):
          tsl = slice(tt * NT, (tt + 1) * NT)
          mo = (l % 2) * 48
          if not go():
              return
          T.lab = "B.q"
          wv, wk = qw
          for hp in range(4):
              ps, pk = proj_fm(wv, wk, hp * 128, lambda kc: HM[:, kc, :], HMall)
              T.op("act", lambda e, hp=hp, ps=ps: e.activation(out=QA[:, hp, :], in_=ps[:, :], func=AF.Identity),
                   reads=[pk], writes=QK(hp))
          if not go():
              return
          T.lab = "B.attn"
          attention(l, tt)
          if not go():
              return
          T.lab = "B.conv+bg"
          for cc in range(4):
              wb = (l * 4 + cc) * 3
              sbuf, sk, _ = nxtS()
              zc = slice(1 + tt * NT, 1 + (tt + 1) * NT)
              zl = slice(tt * NT, (tt + 1) * NT)
              zr = slice(2 + tt * NT, 2 + (tt + 1) * NT)
              zkeys = [("Z", t2, cc) for t2 in (tt - 1, tt, tt + 1) if 0 <= t2 < NTILES] + [("Zpad",)]
              T.op("dve", lambda e, cc=cc, sbuf=sbuf, wb=wb: e.tensor_scalar(
                  out=sbuf[:, :], in0=Z[:, cc, zc], scalar1=CW[:, wb + 1:wb + 2],
                  scalar2=CB[:, l * 4 + cc:l * 4 + cc + 1], op0=ALU.mult, op1=ALU.add),
                  reads=zkeys + [CK(CW), CK(CB)], writes=[sk])
              p1, p1k = nxtPT()
              T.op("dve", lambda e, cc=cc, p1=p1: e.tensor_tensor(out=p1[:, :], in0=Z[:, cc, zl], in1=NF[:, zc],
                                                                  op=ALU.mult),
                   reads=zkeys + [("C", "NF")], writes=[p1k])
              T.op("dve", lambda e, sbuf=sbuf, p1=p1, wb=wb: e.scalar_tensor_tensor(
                  out=sbuf[:, :], in0=p1[:, :], scalar=CW[:, wb:wb + 1], in1=sbuf[:, :], op0=ALU.mult, op1=ALU.add),
                  reads=[p1k, sk, CK(CW)], writes=[sk])
              p2, p2k = nxtPT()
              T.op("dve", lambda e, cc=cc, p2=p2: e.tensor_tensor(out=p2[:, :], in0=Z[:, cc, zr], in1=NF[:, zr],
                                                                  op=ALU.mult),
                   reads=zkeys + [("C", "NF")], writes=[p2k])
              T.op("dve", lambda e, cc=cc, sbuf=sbuf, p2=p2, wb=wb: e.scalar_tensor_tensor(
                  out=QA[:, cc, :], in0=p2[:, :], scalar=CW[:, wb + 2:wb + 3], in1=sbuf[:, :],
                  op0=ALU.mult, op1=ALU.add), reads=[p2k, sk, CK(CW)], writes=QK(cc))
          wv, wk = wloadB(1, wsrc(win_d, l, 0, D, 2048, 512), 8, 512, tt)
          for cc in range(4):
              ps, pk = proj_fm(wv, wk, cc * 128, lambda kc: HM[:, kc, :], HMall)
              T.op("dve", lambda e, cc=cc, ps=ps: e.tensor_tensor(out=QA[:, cc, :], in0=ps[:, :], in1=QA[:, cc, :],
                                                                  op=ALU.mult),
                   reads=[pk] + QK(cc), writes=QK(cc))
          if not go():
              return
          T.lab = "B.gated"
          ATk = [("QA", 4 + i, hf) for i in range(4) for hf in range(2)]
          CTk = [("QA", i, hf) for i in range(4) for hf in range(2)]
          for rnd in range(2):
              gcol0 = 3072 + rnd * 1024
              pw_d = wap_d if rnd == 0 else wcp_d
              src_fn = (lambda kc: QA[:, 4 + kc, :]) if rnd == 0 else (lambda kc: QA[:, kc, :])
              skeys = ATk if rnd == 0 else CTk
              for half in range(2):
                  gv, gk = wloadB(2 + rnd * 2 + half, wsrc(win_d, l, 0, D, gcol0 + half * 512, 512), 8, 512, tt)
                  pv, pvk = wloadB(6 + rnd * 2 + half, wsrc(pw_d, l, 0, 512, half * 512, 512), 4, 512, tt)
                  sgs = []
                  for ol in range(4):
                      ps, pk = proj_fm(gv, gk, ol * 128, lambda kc: HM[:, kc, :], HMall)
                      sg, sgk = nxtPT()
                      T.op("act", lambda e, ps=ps, sg=sg: e.activation(out=sg[:, :], in_=ps[:, :], func=AF.Sigmoid),
                           reads=[pk], writes=[sgk])
                      sgs.append((sg, sgk))
                  for ol in range(4):
                      oc = half * 4 + ol
                      sg, sgk = sgs[ol]
                      ps2, pk2 = proj_fm(pv, pvk, ol * 128, src_fn, skeys, nk=4)
                      if rnd == 0:
                          T.op("dve", lambda e, oc=oc, ps2=ps2, sg=sg: e.tensor_tensor(
                              out=G1[:, oc, :], in0=ps2[:, :], in1=sg[:, :], op=ALU.mult),
                              reads=[pk2, sgk], writes=[("G1", oc)])
                      else:
                          tmp, tk = nxtPT()
                          T.op("dve", lambda e, ps2=ps2, sg=sg, tmp=tmp: e.tensor_tensor(
                              out=tmp[:, :], in0=ps2[:, :], in1=sg[:, :], op=ALU.mult),
                              reads=[pk2, sgk], writes=[tk])
                          T.op("dve", lambda e, oc=oc, tmp=tmp: e.tensor_tensor(
                              out=G1[:, oc, :], in0=G1[:, oc, :], in1=tmp[:, :], op=ALU.add),
                              reads=[tk, ("G1", oc)], writes=[("G1", oc)])
          if not go():
              return
          T.lab = "B.wo"
          G1all = lambda kc: [("G1", kc)]
          for half in range(2):
              wv, wk = wloadB(10 + half, wsrc(wo_d, l, 0, D, half * 512, 512), 8, 512, tt)
              for ol in range(4):
                  oc = half * 4 + ol
                  ps, pk = proj_fm(wv, wk, ol * 128, lambda kc: G1[:, kc, :], G1all)
                  T.op("dve", lambda e, oc=oc, ps=ps: e.scalar_tensor_tensor(
                      out=X[:, oc, tsl], in0=ps[:, :], scalar=MODV[:, mo + 16 + oc:mo + 17 + oc], in1=X[:, oc, tsl],
                      op0=ALU.mult, op1=ALU.add), reads=[pk, ("X", tt, oc), ("MODV", l % 2)],
                      writes=[("X", tt, oc)])

      G1k = lambda c: [("G1", c)]
      HMl = lambda c: [("HM", c)]

      def ln1_prep(l, tt):
          T.lab = "B.ln1"
          ln_prep(l, tt, G1, G1k, QA, QK)

      def ln1_finish(l, tt):
          T.lab = "B.ln1"
          ln_finish(l, tt, L1G, L1B, G1, G1k, QA, QK)

      def passC(l):
          mo = (l % 2) * 48
          if not go():
              return
          T.barrier()
          if l + 1 < DEPTH:
              mod_finish(l + 1)
          T.op("dve", lambda e: e.tensor_tensor(out=GB2[:, :], in0=MODV[:, mo + 40:mo + 48],
                                                in1=B2[:, l * 8:(l + 1) * 8], op=ALU.mult),
               reads=[("MODV", l % 2), CK(B2)], writes=[("C", "GB2")])
          for tt in range(NTILES):
              emit_h(l, tt, 1, lambda c, tt=tt: H2[:, c, tt * NT:(tt + 1) * NT], lambda c, tt=tt: ("H2", tt, c))
              for c in range(8):
                  T.op("act", lambda e, c=c, tt=tt: e.activation(
                      out=X[:, c, tt * NT:(tt + 1) * NT], in_=X[:, c, tt * NT:(tt + 1) * NT], func=AF.Identity,
                      bias=GB2[:, c:c + 1], scale=1.0), reads=[("X", tt, c), ("C", "GB2")], writes=[("X", tt, c)])
          T.lab = "C.mlp"

          def mlp_w1(g, tt, w1v, w1k):
              tsl = slice(tt * NT, (tt + 1) * NT)
              ab = (tt % 2) * 4
              h2k = lambda kc, tt=tt: [("H2", tt, kc)]
              for fc in range(4):
                  ps, pk = proj_fm(w1v, w1k, fc * 128, lambda kc: H2[:, kc, tsl], h2k)
                  r, rk = nxtPT()
                  bi = l * 32 + g * 4 + fc
                  T.op("act", lambda e, ps=ps, r=r, bi=bi: e.activation(
                      out=r[:, :], in_=ps[:, :], func=AF.Relu, bias=B1[:, bi:bi + 1], scale=1.0),
                      reads=[pk, CK(B1)], writes=[rk])
                  T.op("dve", lambda e, r=r, fc=fc, ab=ab: e.tensor_tensor(out=QA[:, ab + fc, :], in0=r[:, :],
                                                                           in1=r[:, :], op=ALU.mult),
                       reads=[rk], writes=QK(ab + fc))

          def mlp_w2(g, tt, w2v, w2k):
              tsl = slice(tt * NT, (tt + 1) * NT)
              ab = (tt % 2) * 4
              ak = [("QA", ab + fc, hf) for fc in range(4) for hf in range(2)]
              for oc in range(8):
                  ps, pk = proj_fm(w2v, w2k, oc * 128, lambda kc: QA[:, ab + kc, :], ak, nk=4)
                  T.op("dve", lambda e, oc=oc, ps=ps, tsl=tsl: e.scalar_tensor_tensor(
                      out=X[:, oc, tsl], in0=ps[:, :], scalar=MODV[:, mo + 40 + oc:mo + 41 + oc],
                      in1=X[:, oc, tsl], op0=ALU.mult, op1=ALU.add),
                      reads=[pk, ("X", tt, oc), ("MODV", l % 2)], writes=[("X", tt, oc)])
              if g == 7:
                  T.lab = "C.ln2"
                  if tt >= 1:
                      ln_finish(l, tt - 1, L2G, L2B, G1, G1k, HM, HMl)
                  ln_prep(l, tt, G1, G1k, HM, HMl)
                  T.lab = "C.mlp"

          for g in range(8):
              w1v, w1k = wload(wsrc(w1_d, l, 0, D, g * 512, 512), 8, 512)
              w2v, w2k = wload(wsrc(w2_d, l, g * 512, 512, 0, D), 4, 1024)
              mlp_w1(g, 0, w1v, w1k)
              for tt in range(NTILES):
                  if tt + 1 < NTILES:
                      mlp_w1(g, tt + 1, w1v, w1k)
                  mlp_w2(g, tt, w2v, w2k)
          T.lab = "C.ln2"
          ln_finish(l, NTILES - 1, L2G, L2B, G1, G1k, HM, HMl)
          T.barrier()

      emit_mod(0)
      for l in range(DEPTH):
          T.dma("pool", lambda q, l=l: q.dma_start(out=KCX[:, :, :], in_=kctx_d[l]), "ctxk", writes=[("KCX",)])
          T.dma("pool", lambda q, l=l: q.dma_start(out=VCX[:, :, :], in_=vctx_d[l]),
                "ctxv", writes=[("VCX",)])
          if l > 0:
              T.op("dve", lambda e: e.memset(V[:, :, :, 64:128], 1.0), writes=[("V", i) for i in range(16)])
          passA2(l, (0, 1))
          passA2(l, (2, 3))
          emit_h(l, 0, 0, lambda c: HM[:, c, :], HMk)
          qw = wloadB(0, wsrc(win_d, l, 0, D, 0, 512), 8, 512, 0)
          for tt in range(NTILES):
              passB(l, tt, qw)
              ln1_prep(l, tt)
              if tt + 1 < NTILES:
                  T.lab = "B.q"
                  emit_h(l, tt + 1, 0, lambda c: HM[:, c, :], HMk)
                  qw = wloadB(0, wsrc(win_d, l, 0, D, 0, 512), 8, 512, tt + 1)
              ln1_finish(l, tt)
          passC(l)
      for c in range(8):
          T.dma("sp", lambda q, c=c: q.dma_start(out=yT_o[c * 128:(c + 1) * 128, :], in_=X[:, c, :]), "yout",
                reads=[("X", t, c) for t in range(NTILES)])
      for sk in ("yout", "oS0", "oS1", "oS2", "oS3", "oS4"):
          v = T.dma_cnt.get(sk, 0)
          if v and T.target == "sp":
              T.eng.wait_ge(T.sem[sk], v)

    if DRY[0]:
        T0 = Trk(None, None, SEM)
        schedule(T0)
        DRY[0] = T0
        return nc
    with nc.Block() as block:
        @block.tensor
        def _(e):
            schedule(Trk("pe", e, SEM))

        @block.scalar
        def _(e):
            schedule(Trk("act", e, SEM))

        @block.vector
        def _(e):
            schedule(Trk("dve", e, SEM))

        @block.gpsimd
        def _(e):
            schedule(Trk("pool", e, SEM))

        @block.sync
        def _(e):
            schedule(Trk("sp", e, SEM))
    print("sbuf bytes remaining", nc.sbuf_bytes_remaining)
    es.close()
    return nc


def _pc(v, nchunk):
    Ld = v.shape[0]
    return np.ascontiguousarray(v.reshape(Ld, nchunk, 128).transpose(2, 0, 1).reshape(128, Ld * nchunk))


def _core_tables(role, rpb):
    ew = np.zeros((128, 4, 8, 8), np.float32)
    a = (np.arange(128) // 64)[:, None, None, None]
    tt = np.arange(4)[None, :, None, None]
    c = np.arange(8)[None, None, :, None]
    j = np.arange(8)[None, None, None, :]
    kr = 8 * tt - 4 + 2 * c + a
    qr = 8 * tt + j
    inb = (kr >= 0) & (kr < 32)
    if role == "sample":
        sr = np.clip(qr - 4, 0, 24)
        valid = inb & (kr >= sr) & (kr < sr + 8)
    else:
        valid = inb & ((kr // 4) == (qr // 4))
    ew[:] = np.broadcast_to(valid, ew.shape).astype(np.float32)
    nfm = np.zeros((1, NTOK + 2), np.float32)
    t = np.arange(NTOK)
    if role == "sample":
        nfm[0, 1:NTOK + 1] = (t != 0)
    else:
        nfm[0, 1:NTOK + 1] = (t % 256 != 0)
    p = np.arange(128)
    aa = (p // 64)[:, None, None]
    kc = (p % 64)[:, None, None]
    tcol = (np.arange(NTM) + TMIN)[None, :, None]
    qc = np.arange(64)[None, None, :]
    dr = 17 + aa - tcol
    cs = np.clip(qc - 8, 0, 48)
    cm = (kc >= cs) & (kc < cs + 16)
    ok = (dr >= 0) & (dr <= 14) & cm
    dc = np.clip(kc - qc + 15, 0, 30)
    flat_idx = np.where(ok, np.clip(dr, 0, 14) * 31 + dc, 15 * 31)
    flat_idx = np.broadcast_to(flat_idx, (128, NTM, 64)).reshape(128, MW)
    padval = NEG if role == "sample" else 0.0
    rp = np.concatenate([rpb.reshape(DEPTH, H, 15 * 31),
                         np.full((DEPTH, H, 1), padval, np.float32)], axis=2)
    if role == "sample":
        mtab = rp[:, :, flat_idx]
    else:
        mtab = rp[:, :, np.full_like(flat_idx, 15 * 31)]
    return ew.reshape(128, 256), np.ascontiguousarray(mtab.astype(np.float32)), nfm


_NC_CACHE = {}
_HOOK = {}


def kernel(x_prompt, x_sample, cache_k, cache_v, c, c_ctx, w_mod, b_mod, w_in, rpb, conv_w, conv_b,
           w_attn_proj, w_conv_proj, w_o, ln1_g, ln1_b, w1, b1, w2, b2, ln2_g, ln2_b):
    f = lambda a: np.ascontiguousarray(np.asarray(a, dtype=np.float32))
    x_prompt, x_sample, cache_k, cache_v, c, c_ctx = map(f, (x_prompt, x_sample, cache_k, cache_v, c, c_ctx))
    rpb = f(rpb)
    shared = {
        "w_mod": f(w_mod), "bmod_pc": _pc(f(b_mod), 48), "w_in": f(w_in),
        "convw_pc": np.ascontiguousarray(
            f(conv_w).reshape(DEPTH, 3, 4, 128).transpose(3, 0, 2, 1).reshape(128, DEPTH * 12)),
        "convb_pc": _pc(f(conv_b), 4),
        "w_attn_proj": f(w_attn_proj), "w_conv_proj": f(w_conv_proj), "w_o": f(w_o),
        "ln1g_pc": _pc(f(ln1_g), 8), "ln1b_pc": _pc(f(ln1_b), 8), "w1": f(w1), "b1_pc": _pc(f(b1), 32),
        "w2": f(w2), "b2_pc": _pc(f(b2), 8), "ln2g_pc": _pc(f(ln2_g), 8), "ln2b_pc": _pc(f(ln2_b), 8),
    }
    tabs = {r: _core_tables(r, rpb) for r in ("sample", "prompt")}
    in_maps = []
    roles = []
    for core in range(NCORES):
        if core < 2:
            role, b = "sample", core
            xs = x_sample[b]
            cond = c[b]
            kc_ = cache_k[b]
            kctxT = np.ascontiguousarray(kc_.reshape(DEPTH, 4, 2, 256, 64).transpose(0, 2, 4, 1, 3)
                                         .reshape(DEPTH, 128, 4, 256))
            vc_ = cache_v[b]
            vctx = np.ones((DEPTH, 128, 2, 4, 3, 64), np.float32)
            vctx[:, :, :, :, 0:3:2, :] = vc_.reshape(DEPTH, 4, 2, 2, 128, 64).transpose(0, 4, 3, 1, 2, 5)
            vctx = vctx.reshape(DEPTH, 128, 2, 768)
            ctxb = np.zeros((128, 1), np.float32)
        else:
            role = "prompt"
            pc = min(core - 2, 3)
            xs = x_prompt[pc * 8:(pc + 1) * 8].reshape(NTOK, D)
            cond = c_ctx
            kctxT = np.zeros((DEPTH, 128, 4, 256), np.float32)
            vctx = np.zeros((DEPTH, 128, 2, 768), np.float32)
            ctxb = np.full((128, 1), NEG, np.float32)
        ew, mtab, nfm = tabs[role]
        m = dict(shared)
        m.update({"xT": np.ascontiguousarray(xs.T), "cond_pc": np.ascontiguousarray(cond.reshape(8, 128).T),
                  "kctxT": kctxT, "vctx": vctx, "ctxb": ctxb, "ew": ew, "mtab": mtab, "nfm": nfm})
        in_maps.append(m)
        roles.append(role)
    if _HOOK.get("in_maps_only"):
        return in_maps
    if "nc" not in _NC_CACHE:
        _NC_CACHE["nc"] = build_program()
    nc = _NC_CACHE["nc"]
    res = run_bass_kernel_spmd(nc, in_maps, core_ids=list(range(NCORES)))
    R = res.results
    y_sample = np.stack([R[b]["yT"].T for b in range(2)], axis=0).astype(np.float32)
    y_prompt = np.concatenate([R[2 + pc]["yT"].T.reshape(8, 256, D) for pc in range(4)], axis=0).astype(np.float32)
    nk = np.concatenate([R[2 + pc]["kT_out"].reshape(DEPTH, H, HD, 8, 256).transpose(3, 0, 1, 4, 2)
                         for pc in range(4)], axis=0)
    nv = np.concatenate([R[2 + pc]["v_out"].reshape(DEPTH, 8, 256, H, HD).transpose(1, 0, 3, 2, 4)
                         for pc in range(4)], axis=0)
    return (np.ascontiguousarray(y_prompt), np.ascontiguousarray(y_sample),
            np.ascontiguousarray(nk.astype(np.float32)), np.ascontiguousarray(nv.astype(np.float32)))
```

```python
import numpy as np
from contextlib import ExitStack
import concourse.bass as bass
import concourse.mybir as mybir
from concourse.bass_utils import run_bass_kernel_spmd

F32 = mybir.dt.float32
BF16 = mybir.dt.bfloat16
AF = mybir.ActivationFunctionType
ALU = mybir.AluOpType

D = 1024
DEPTH = 4
NTOK = 2048
NT = 512
NTILES = 4
H = 8
HD = 64
DFF = 4096
INC = 5120
SCALE = HD ** -0.5
ALPHA = (2.0 * DEPTH) ** 0.25
LN_EPS = 1e-5
EPS_P = LN_EPS / (ALPHA * ALPHA)
NEG = -30000.0
NCORES = 8
JR = {0: (0, 2), 1: (0, 4), 2: (0, 8), 3: (0, 8), 4: (0, 8), 5: (0, 8), 6: (5, 8), 7: (7, 8)}
TMIN = 4
NTM = 14
MW = NTM * 64
ENGS = ("pe", "act", "dve", "pool", "sp")
DRY = [False]
LIMIT = [10 ** 9]
DBG = [99]
HENG = ["dve"]
LNENG = ["dve"]
EWPOOL = [()]
SPILL = [True]


class Trk:
    def __init__(self, target, eng, sem):
        self.target = target
        self.eng = eng
        self.cnt = {e: 0 for e in ENGS}
        self.known = {e: {} for e in ENGS}
        self.last_w = {}
        self.readers = {}
        self.dma_cnt = {}
        self.sem = sem
        self.rec = {e: [] for e in ENGS} if target is None else None
        self.lab = ""
        self.pelabs = []

    def _need(self, reads, writes):
        need = {}

        def add(k, v):
            if need.get(k, 0) < v:
                need[k] = v
        for r in reads:
            lw = self.last_w.get(r)
            if lw:
                add(*lw)
            if r[0] in ("PSG", "PSA"):
                for k, v in self.readers.get(r, {}).items():
                    add(k, v)
        for w in writes:
            lw = self.last_w.get(w)
            if lw:
                add(*lw)
            for k, v in self.readers.get(w, {}).items():
                add(k, v)
        return need

    def _waits(self, e, need):
        for k, v in need.items():
            if k == e and e == "pe":
                continue
            if self.known[e].get(k, 0) >= v:
                continue
            self.known[e][k] = v
            if self.rec is not None:
                self.rec[e].append(("wait", k, v))
            if e == self.target:
                self.eng.wait_ge(self.sem[k], v)

    def op(self, e, fn, reads=(), writes=()):
        self._waits(e, self._need(reads, writes))
        self.cnt[e] += 1
        idx = self.cnt[e]
        if self.rec is not None:
            self.rec[e].append(("inc", e, 1))
            if e == "pe":
                self.pelabs.append(self.lab)
        if e == self.target:
            fn(self.eng).then_inc(self.sem[e], 1)
        for r in reads:
            self.readers.setdefault(r, {})[e] = idx
        for w in writes:
            self.last_w[w] = (e, idx)
            self.readers[w] = {}

    def dma(self, q, fn, semkey, reads=(), writes=()):
        self._waits(q, self._need(reads, writes))
        self.dma_cnt[semkey] = self.dma_cnt.get(semkey, 0) + 16
        val = self.dma_cnt[semkey]
        if self.rec is not None:
            self.rec[q].append(("inc", semkey, 16))
        if q == self.target:
            fn(self.eng).then_inc(self.sem[semkey], 16)
        for r in reads:
            self.readers.setdefault(r, {})[semkey] = val
        for w in writes:
            self.last_w[w] = (semkey, val)
            self.readers[w] = {}

    def fix_batch(self, semkey):
        val = self.dma_cnt[semkey]
        for k, (sk, v) in list(self.last_w.items()):
            if sk == semkey:
                self.last_w[k] = (sk, val)

    def barrier(self):
        tot = {e: self.cnt[e] for e in ("pe", "act", "dve", "pool") if self.cnt[e] > 0}
        for sk, v in self.dma_cnt.items():
            tot[sk] = v
        for e in ("pe", "act", "dve", "pool"):
            need = {k: v for k, v in tot.items() if k != e}
            self._waits(e, need)


def _jr_per_tile():
    dummy = np.zeros((DEPTH, H, 15, 31), np.float32)
    ews = [_core_tables(r, dummy)[0].reshape(128, 4, 8, 8) for r in ("sample", "prompt")]
    out = []
    for tt in range(NTILES):
        row = {}
        for c in range(8):
            v = np.zeros(8, bool)
            for ew in ews:
                v |= (ew[:, tt, c, :] != 0).any(axis=0)
            js = np.nonzero(v)[0]
            row[c] = (int(js[0]), int(js[-1]) + 1) if len(js) else None
        out.append(row)
    return out


def build_program():
    global JRT
    JRT = _jr_per_tile()
    nc = bass.Bass("TRN2", target_bir_lowering=False)
    SEM = {}

    def din(name, shape, dt=F32):
        return nc.dram_tensor(name, list(shape), dt, kind="ExternalInput").ap()

    def dout(name, shape):
        return nc.dram_tensor(name, list(shape), F32, kind="ExternalOutput").ap()

    xT_d = din("xT", [D, NTOK])
    cond_d = din("cond_pc", [128, 8])
    kctx_d = din("kctxT", [DEPTH, 128, 4, 256])
    vctx_d = din("vctx", [DEPTH, 128, 2, 768])
    ctxb_d = din("ctxb", [128, 1])
    ew_d = din("ew", [128, 256])
    mtab_d = din("mtab", [DEPTH, H, 128, MW])
    nf_d = din("nfm", [1, NTOK + 2])
    wmod_d = din("w_mod", [DEPTH, D, 6 * D])
    bmod_d = din("bmod_pc", [128, DEPTH * 48])
    win_d = din("w_in", [DEPTH, D, INC])
    cw_d = din("convw_pc", [128, DEPTH * 4 * 3])
    cb_d = din("convb_pc", [128, DEPTH * 4])
    wap_d = din("w_attn_proj", [DEPTH, 512, D])
    wcp_d = din("w_conv_proj", [DEPTH, 512, D])
    wo_d = din("w_o", [DEPTH, D, D])
    l1g_d = din("ln1g_pc", [128, DEPTH * 8])
    l1b_d = din("ln1b_pc", [128, DEPTH * 8])
    w1_d = din("w1", [DEPTH, D, DFF])
    b1_d = din("b1_pc", [128, DEPTH * 32])
    w2_d = din("w2", [DEPTH, DFF, D])
    b2_d = din("b2_pc", [128, DEPTH * 8])
    l2g_d = din("ln2g_pc", [128, DEPTH * 8])
    l2b_d = din("ln2b_pc", [128, DEPTH * 8])

    yT_o = dout("yT", [D, NTOK])
    kT_o = dout("kT_out", [DEPTH, 512, NTOK])
    v_o = dout("v_out", [DEPTH, NTOK, 512])

    SCR = nc.dram_tensor("scr_bf16", [16, 128, 4096], BF16).ap()
    es = ExitStack()

    def sb(name, shape, dt):
        return es.enter_context(nc.sbuf_tensor(name, list(shape), dt))

    X = sb("X", [128, 8, NTOK], F32)
    BIG = sb("BIG", [128, 20480], BF16)
    Z = sb("Z", [128, 4, NTOK + 2], BF16)
    HM = sb("HM", [128, 8, NT], BF16)
    G1 = sb("G1", [128, 8, NT], BF16)
    QA = sb("QA", [128, 8, NT], BF16)
    S = [sb(f"S{i}", [128, NT], F32) for i in range(2)]
    PT = [sb(f"PT{i}", [128, NT], BF16) for i in range(5)]
    MS = [sb(f"MS{i}", [128, MW], F32) for i in range(2)]
    EM = [sb(f"EM{i}", [128, MW], BF16) for i in range(2)]
    EWB = sb("EWB", [128, 256], BF16)
    RING = [sb(f"RG{i}", [128, 4096], BF16) for i in range(3)]
    L = [sb(f"L{i}", [128, NT], F32) for i in range(3)]
    NF = sb("NF", [128, NTOK + 2], BF16)
    KCX = sb("KCX", [128, 4, 256], BF16)
    VCX = sb("VCX", [128, 2, 768], BF16)
    MODV = sb("MODV", [128, 2 * 48], F32)
    BMOD = sb("BMOD", [128, DEPTH * 48], F32)
    L1G = sb("L1G", [128, DEPTH * 8], F32)
    L1B = sb("L1B", [128, DEPTH * 8], F32)
    L2G = sb("L2G", [128, DEPTH * 8], F32)
    L2B = sb("L2B", [128, DEPTH * 8], F32)
    B1 = sb("B1", [128, DEPTH * 32], F32)
    B2 = sb("B2", [128, DEPTH * 8], F32)
    GB2 = sb("GB2", [128, 8], F32)
    CW = sb("CW", [128, DEPTH * 12], F32)
    CB = sb("CB", [128, DEPTH * 4], F32)
    EW = sb("EW", [128, 256], F32)
    ONES = sb("ONES", [128, 128], BF16)
    CONDT = sb("CONDT", [128, 8], F32)
    SCB = sb("SCB", [128, 8], BF16)
    CTXB = sb("CTXB", [128, 1], F32)
    EPSC = sb("EPSC", [128, 1], F32)
    ONEF = sb("ONEF", [128, 1], F32)

    PSG = [es.enter_context(nc.psum_tensor(f"PSG{i}", [128, NT], F32)) for i in range(5)]
    PSA = [es.enter_context(nc.psum_tensor(f"PSA{i}", [128, NT], F32)) for i in range(2)]
    PSM = es.enter_context(nc.psum_tensor("PSM", [128, NT], F32))

    semnames = ["pe", "act", "dve", "pool", "ring0", "ring1", "ring2", "xin", "cst", "m0", "m1",
                "ctxk", "ctxv", "nfs", "sp0", "sp1", "sp2", "oS0", "oS1", "oS2", "oS3", "oS4", "yout"]
    for n in semnames:
        SEM[n] = es.enter_context(nc.semaphore("s_" + n))

    KT = BIG[:, 0:8192].rearrange("p (h t) -> p h t", h=4)
    V = BIG[:, 8192:20480].rearrange("p (c h f) -> p c h f", c=16, h=4)
    H2 = BIG[:, 0:16384].rearrange("p (c t) -> p c t", c=8)
    VCX4 = VCX[:, :, :].rearrange("p c (h f) -> p c h f", h=4)

    def schedule(T):
      st = {"gen": 0, "acc": 0, "ring": 0, "S": 0, "PT": 0, "stage": 0, "stg": 0}

      def go():
          st["stage"] += 1
          return st["stage"] <= LIMIT[0]

      def psg():
          i = st["gen"] % 5
          st["gen"] += 1
          return PSG[i], ("PSG", i)

      def psa():
          i = st["acc"] % 2
          st["acc"] += 1
          return PSA[i], ("PSA", i)

      def nxtS():
          i = st["S"] % 2
          st["S"] += 1
          return S[i], ("S", i), i

      STG = [(S[0], ("S", 0)), (S[1], ("S", 1)), (L[0], ("L", 0)), (L[1], ("L", 1)), (L[2], ("L", 2))]

      def nxtStg():
          i = st["stg"] % 5
          st["stg"] += 1
          return STG[i][0], STG[i][1], i

      def nxtPT():
          i = st["PT"] % 5
          st["PT"] += 1
          return PT[i], ("PT", i)

      def wload(src_ap, kk, ncols):
          i = st["ring"] % 3
          st["ring"] += 1
          view = RING[i][:, 0:kk * ncols].rearrange("p (k n) -> p k n", k=kk)
          T.dma("pool", lambda q, view=view, src_ap=src_ap: q.dma_start(out=view, in_=src_ap),
                f"ring{i}", writes=[("RG", i)])
          return view, ("RG", i)

      def wsrc(wd, l, r0, nrows, c0, ncols):
          return wd[l, r0:r0 + nrows, c0:c0 + ncols].rearrange("(k p) n -> p k n", p=128)

      def wloadB(gi, src_ap, kk, ncols, tt):
          if not SPILL[0]:
              return wload(src_ap, kk, ncols)
          n = kk * ncols
          if tt == 0:
              view, key = wload(src_ap, kk, ncols)
              i = key[1]
              T.dma("sp", lambda q, i=i, gi=gi, n=n: q.dma_start(out=SCR[gi, :, 0:n], in_=RING[i][:, 0:n]),
                    f"sp{i}", reads=[key], writes=[("SCR", gi)])
              return view, key
          i = st["ring"] % 3
          st["ring"] += 1
          view = RING[i][:, 0:n].rearrange("p (k n) -> p k n", k=kk)
          T.dma("pool", lambda q, i=i, gi=gi, n=n: q.dma_start(out=RING[i][:, 0:n], in_=SCR[gi, :, 0:n]),
                f"ring{i}", reads=[("SCR", gi)], writes=[("RG", i)])
          return view, ("RG", i)

      for c in range(8):
          T.dma("sp", lambda q, c=c: q.dma_start(out=X[:, c, :], in_=xT_d[c * 128:(c + 1) * 128, :]),
                "xin", writes=[("X", t, c) for t in range(NTILES)])
      T.fix_batch("xin")
      smalls = [(CONDT, cond_d), (BMOD, bmod_d), (L1G, l1g_d), (L1B, l1b_d), (L2G, l2g_d), (L2B, l2b_d),
                (B1, b1_d), (B2, b2_d), (CW, cw_d), (CB, cb_d), (EW, ew_d), (CTXB, ctxb_d)]
      for t_sb, t_d in smalls:
          T.dma("sp", lambda q, t_sb=t_sb, t_d=t_d: q.dma_start(out=t_sb[:, :], in_=t_d), "cst",
                writes=[("C", t_sb.name)])
      T.dma("pool", lambda q: q.dma_start(out=NF[:, :], in_=nf_d.partition_broadcast(128)), "nfs",
            writes=[("C", "NF")])
      T.fix_batch("cst")
      CK = lambda t: ("C", t.name)

      T.op("dve", lambda e: e.memset(ONES[:, :], 1.0), writes=[("C", "ONES")])
      T.op("dve", lambda e: e.memset(EPSC[:, :], EPS_P), writes=[("C", "EPSC")])
      T.op("dve", lambda e: e.memset(ONEF[:, :], 1.0), writes=[("C", "ONEF")])
      T.op("dve", lambda e: e.memset(Z[:, :, :], 0.0), writes=[("Z", t) for t in range(NTILES)] + [("Zpad",)])
      T.op("dve", lambda e: e.memset(BIG[:, 8192:20480], 1.0), writes=[("V", i) for i in range(16)])
      T.op("dve", lambda e: e.memset(VCX[:, :, :], 1.0), writes=[("VCX",)])
      T.op("dve", lambda e: e.tensor_copy(out=EWB[:, :], in_=EW[:, :]), reads=[CK(EW)], writes=[("C", "EWB")])
      T.op("act", lambda e: e.activation(out=SCB[:, :], in_=CONDT[:, :], func=AF.Silu),
           reads=[CK(CONDT)], writes=[("C", "SCB")])

      pcol, pck = PSM, ("PSA", 9)

      def mod_load(l, g):
          return wload(wsrc(wmod_d, l, 0, D, g * 512, 512), 8, 512)

      def mod_group(l, g, pre=None):
          lab0 = T.lab
          T.lab = f"L{l}.mod"
          wv, wk = pre if pre is not None else mod_load(l, g)
          prow, prk = psg()
          for kc in range(8):
              T.op("pe", lambda e, wv=wv, kc=kc, prow=prow: e.matmul(
                  prow[0:1, :], lhsT=SCB[:, kc:kc + 1], rhs=wv[:, kc, :], start=(kc == 0), stop=(kc == 7)),
                  reads=[wk, ("C", "SCB")], writes=[prk])
          rhi, rhk = nxtPT()
          rlo, rlk = nxtPT()
          T.op("act", lambda e, rhi=rhi, prow=prow: e.activation(out=rhi[0:1, :], in_=prow[0:1, :],
                                                                   func=AF.Identity),
               reads=[prk], writes=[rhk])
          T.op("dve", lambda e, rhi=rhi, rlo=rlo, prow=prow: e.tensor_tensor(
              out=rlo[0:1, :], in0=prow[0:1, :], in1=rhi[0:1, :], op=ALU.subtract),
              reads=[prk, rhk], writes=[rlk])
          for fl in range(4):
              col = g * 4 + fl
              T.op("pe", lambda e, rhi=rhi, fl=fl, col=col: e.matmul(
                  pcol[:, col:col + 1], lhsT=rhi[0:1, fl * 128:(fl + 1) * 128], rhs=ONES[0:1, 0:1],
                  start=True, stop=False), reads=[rhk, ("C", "ONES")], writes=[pck])
              T.op("pe", lambda e, rlo=rlo, fl=fl, col=col: e.matmul(
                  pcol[:, col:col + 1], lhsT=rlo[0:1, fl * 128:(fl + 1) * 128], rhs=ONES[0:1, 0:1],
                  start=False, stop=True), reads=[rlk, ("C", "ONES")], writes=[pck])
          T.lab = lab0

      def mod_finish(l):
          mo = (l % 2) * 48
          mk = ("MODV", l % 2)
          T.op("dve", lambda e: e.tensor_tensor(out=MODV[:, mo:mo + 48], in0=pcol[:, 0:48],
                                                in1=BMOD[:, l * 48:(l + 1) * 48], op=ALU.add),
               reads=[pck, CK(BMOD)], writes=[mk])
          for a_ in (8, 32):
              T.op("dve", lambda e, a_=a_: e.tensor_scalar_add(out=MODV[:, mo + a_:mo + a_ + 8],
                                                              in0=MODV[:, mo + a_:mo + a_ + 8], scalar1=1.0),
                   reads=[mk], writes=[mk])
          for a_ in (16, 40):
              T.op("dve", lambda e, a_=a_: e.tensor_scalar_mul(out=MODV[:, mo + a_:mo + a_ + 8],
                                                              in0=MODV[:, mo + a_:mo + a_ + 8], scalar1=1.0 / ALPHA),
                   reads=[mk], writes=[mk])

      def emit_mod(l):
          if not go():
              return
          for g in range(12):
              mod_group(l, g)
          mod_finish(l)

      def emit_h(l, tt, which, dst_fn, dkeys):
          mo = (l % 2) * 48 + which * 24
          for c in range(8):
              T.op(HENG[0], lambda e, c=c: e.tensor_scalar(
                  out=dst_fn(c), in0=X[:, c, tt * NT:(tt + 1) * NT], scalar1=MODV[:, mo + 8 + c:mo + 8 + c + 1],
                  scalar2=MODV[:, mo + c:mo + c + 1], op0=ALU.mult, op1=ALU.add),
                  reads=[("X", tt, c), ("MODV", l % 2)],
                  writes=(dkeys(c) if isinstance(dkeys(c), list) else [dkeys(c)]))

      def proj_fm(wv, wk, ncol0, rhs_fn, rkeys, nk=8):
          ps, pk = psg()
          for kc in range(nk):
              rk_ = rkeys(kc) if callable(rkeys) else rkeys
              T.op("pe", lambda e, kc=kc: e.matmul(ps[:, :], lhsT=wv[:, kc, ncol0:ncol0 + 128], rhs=rhs_fn(kc),
                                                   start=(kc == 0), stop=(kc == nk - 1)),
                   reads=[wk] + rk_, writes=[pk])
          return ps, pk

      def out_dma(src_ap, skey, si, dst_ap):
          T.dma("sp", lambda q: q.dma_start(out=dst_ap, in_=src_ap), f"oS{si}", reads=[skey])

      QK = lambda i: [("QA", i, 0), ("QA", i, 1)]
      HMk = lambda c: ("HM", c)
      HMall = lambda kc: [("HM", kc)]

      def passA2(l, tiles):
          if not go():
              return
          T.lab = "A.kvuc"
          hb = {}
          for i, tt in enumerate(tiles):
              if i == 0:
                  emit_h(l, tt, 0, lambda c: HM[:, c, :], HMk)
                  hb[tt] = (HM, lambda kc: [("HM", kc)])
              else:
                  emit_h(l, tt, 0, lambda c: QA[:, c, :], QK)
                  hb[tt] = (QA, lambda kc: [("QA", kc, 0), ("QA", kc, 1)])
          wv, wk = wloadB(12, wsrc(win_d, l, 0, D, 512, 512), 8, 512, tiles[0] // 2)
          for tt in tiles:
              HB, HK = hb[tt]
              tsl = slice(tt * NT, (tt + 1) * NT)
              for hp in range(4):
                  ps, pk = proj_fm(wv, wk, hp * 128, lambda kc, HB=HB: HB[:, kc, :], HK)
                  sbuf, sk, si = nxtStg()
                  T.op("act", lambda e, ps=ps, sbuf=sbuf: e.activation(out=sbuf[:, :], in_=ps[:, :],
                                                                       func=AF.Identity), reads=[pk], writes=[sk])
                  T.op("dve", lambda e, hp=hp, sbuf=sbuf, tsl=tsl: e.tensor_copy(out=KT[:, hp, tsl], in_=sbuf[:, :]),
                       reads=[sk], writes=[("KT", tt, hp)])
                  out_dma(sbuf[:, :], sk, si, kT_o[l, hp * 128:(hp + 1) * 128, tsl])
          wv, wk = wloadB(13, wsrc(win_d, l, 0, D, 1024, 512), 8, 512, tiles[0] // 2)
          for tt in tiles:
              HB, HK = hb[tt]
              for tb in range(4):
                  ps, pk = psg()
                  for kc in range(8):
                      T.op("pe", lambda e, kc=kc, tb=tb, ps=ps, wv=wv, HB=HB: e.matmul(
                          ps[:, :], lhsT=HB[:, kc, tb * 128:(tb + 1) * 128], rhs=wv[:, kc, :],
                          start=(kc == 0), stop=(kc == 7)), reads=[wk] + HK(kc), writes=[pk])
                  ci = tt * 4 + tb
                  sbuf, sk, si = nxtStg()
                  T.op("act", lambda e, ps=ps, sbuf=sbuf: e.activation(out=sbuf[:, :], in_=ps[:, :],
                                                                       func=AF.Identity), reads=[pk], writes=[sk])
                  T.op("dve", lambda e, ci=ci, sbuf=sbuf: e.tensor_copy(
                      out=V[:, ci, :, :].rearrange("p h (s f) -> p h s f", s=3)[:, :, 0:3:2, :],
                      in_=sbuf[:, :].rearrange("p (h s f) -> p h s f", h=4, s=2)),
                      reads=[sk], writes=[("V", ci)])
                  out_dma(sbuf[:, :], sk, si, v_o[l, tt * NT + tb * 128:tt * NT + (tb + 1) * 128, :])
          wv, wk = wloadB(14, wsrc(win_d, l, 0, D, 1536, 512), 8, 512, tiles[0] // 2)
          for tt in tiles:
              HB, HK = hb[tt]
              zsl = slice(1 + tt * NT, 1 + (tt + 1) * NT)
              for cc in range(4):
                  ps, pk = proj_fm(wv, wk, cc * 128, lambda kc, HB=HB: HB[:, kc, :], HK)
                  T.op("act", lambda e, cc=cc, ps=ps, zsl=zsl: e.activation(out=Z[:, cc, zsl], in_=ps[:, :],
                                                                            func=AF.Identity),
                       reads=[pk], writes=[("Z", tt, cc)])
          wv, wk = wloadB(15, wsrc(win_d, l, 0, D, 2560, 512), 8, 512, tiles[0] // 2)
          for tt in tiles:
              HB, HK = hb[tt]
              zsl = slice(1 + tt * NT, 1 + (tt + 1) * NT)
              for cc in range(4):
                  ps, pk = proj_fm(wv, wk, cc * 128, lambda kc, HB=HB: HB[:, kc, :], HK)
                  T.op("dve", lambda e, cc=cc, ps=ps, zsl=zsl: e.tensor_tensor(out=Z[:, cc, zsl], in0=ps[:, :],
                                                                               in1=Z[:, cc, zsl], op=ALU.mult),
                       reads=[pk, ("Z", tt, cc)], writes=[("Z", tt, cc)])

      def attention(l, tt):
          modpre = [None, None]
          JRt = JRT[tt]
          for h in range(H):
              hp, half = h // 2, h % 2
              pb = 64 * half
              em = EM[h % 2]
              emk = ("EM", h % 2)

              def load_em(hh):
                  ms_ = MS[hh % 2]
                  em_ = EM[hh % 2]
                  T.dma("sp", lambda q: q.dma_start(out=ms_[:, :], in_=mtab_d[l, hh]), f"m{hh % 2}",
                        writes=[("MS", hh % 2)])
                  T.op("act", lambda e: e.activation(out=em_[:, :], in_=ms_[:, :], func=AF.Exp),
                       reads=[("MS", hh % 2)], writes=[("EM", hh % 2)])
              if h == 0:
                  load_em(0)
              if h + 1 < H:
                  load_em(h + 1)
              acc, ak = psa()
              order = [c for c in (3, 2, 4, 5, 0, 1, 6, 7) if 0 <= tt * 4 - 2 + c < 16 and JRt[c] is not None]
              assert JRt[3] == (0, 8)
              steps = [("b", c) for c in order]
              steps.insert(1, ("x", 0))
              steps.insert(4, ("x", 1))
              qkey = ("QA", hp, half)
              pend = []

              def do_pv(item, first, last):
                  kind, c, pt, ptk, j0, j1 = item
                  ncols = (j1 - j0) * 64
                  if kind == "b":
                      ci = tt * 4 - 2 + c
                      lhs = V[:, ci, hp, 64 * half:64 * half + 128]
                      vk = ("V", ci)
                  else:
                      lhs = VCX4[:, c, hp, 64 * half:64 * half + 128]
                      vk = ("VCX",)
                  T.op("pe", lambda e: e.matmul(acc[:, j0 * 64:j1 * 64], lhsT=lhs, rhs=pt[:, 0:ncols],
                                                start=first, stop=last),
                       reads=[ptk, vk], writes=[ak])

              n_done = 0
              for si_, (kind, c) in enumerate(steps):
                  ps, pk = psg()
                  pt, ptk = nxtPT()
                  if kind == "b":
                      j0, j1 = JRt[c]
                      ncols = (j1 - j0) * 64
                      g0 = tt * NT - 256 + c * 128
                      T.op("pe", lambda e, ps=ps, g0=g0, j0=j0, j1=j1, ncols=ncols: e.matmul(
                          ps[:, 0:ncols], lhsT=KT[pb:pb + 64, hp, g0:g0 + 128],
                          rhs=QA[pb:pb + 64, hp, j0 * 64:j1 * 64], start=True, stop=True),
                          reads=[("KT", g0 // NT, hp), qkey], writes=[pk])
                      T.op("act", lambda e, ps=ps, pt=pt, ncols=ncols: e.activation(
                          out=pt[:, 0:ncols], in_=ps[:, 0:ncols], func=AF.Exp, scale=SCALE),
                          reads=[pk], writes=[ptk])
                      t0 = j0 - 2 * c + 14
                      m0 = (t0 - TMIN) * 64
                      T.op("dve", lambda e, pt=pt, ncols=ncols, m0=m0: e.tensor_tensor(
                          out=pt[:, 0:ncols], in0=pt[:, 0:ncols], in1=em[:, m0:m0 + ncols], op=ALU.mult),
                          reads=[ptk, emk], writes=[ptk])
                      eb = (tt * 8 + c) * 8
                      nj = j1 - j0
                      T.op("pool" if c in EWPOOL[0] else "dve", lambda e, pt=pt, nj=nj, eb=eb, j0=j0: e.tensor_tensor(
                          out=pt[:, 0:nj * 64].rearrange("p (j q) -> p j q", j=nj),
                          in0=pt[:, 0:nj * 64].rearrange("p (j q) -> p j q", j=nj),
                          in1=EWB[:, eb + j0:eb + j0 + nj].unsqueeze(2).to_broadcast([128, nj, 64]), op=ALU.mult),
                          reads=[ptk, ("C", "EWB")], writes=[ptk])
                  else:
                      j0, j1 = 0, 8
                      T.op("pe", lambda e, ps=ps, c=c: e.matmul(
                          ps[:, :], lhsT=KCX[pb:pb + 64, hp, c * 128:(c + 1) * 128],
                          rhs=QA[pb:pb + 64, hp, :], start=True, stop=True),
                          reads=[("KCX",), qkey], writes=[pk])
                      T.op("act", lambda e, ps=ps, pt=pt: e.activation(
                          out=pt[:, :], in_=ps[:, :], func=AF.Exp, bias=CTXB[:, 0:1], scale=SCALE),
                          reads=[pk, CK(CTXB)], writes=[ptk])
                  pend.append((kind, c, pt, ptk, j0, j1))
                  if len(pend) > 3:
                      do_pv(pend.pop(0), n_done == 0, False)
                      n_done += 1
              while pend:
                  do_pv(pend.pop(0), n_done == 0, len(pend) == 0)
                  n_done += 1
              if l + 1 < DEPTH:
                  if h in (0, 2, 4):
                      modpre[0] = mod_load(l + 1, tt * 3 + h // 2)
                  if h in (2, 4, 6):
                      mod_group(l + 1, tt * 3 + (h - 2) // 2, modpre[1])
                  if h in (0, 2, 4, 6):
                      modpre[1] = modpre[0]
              rd = L[h % 2]
              rk = ("L", h % 2)
              db = 64 - pb
              T.op("act", lambda e, rd=rd: e.activation(out=rd[db:db + 64, :], in_=acc[db:db + 64, :], func=AF.Ln),
                   reads=[ak], writes=[rk])
              T.op("act", lambda e, rd=rd: e.activation(out=rd[db:db + 64, :], in_=rd[db:db + 64, :], func=AF.Exp,
                                                        scale=-1.0), reads=[rk], writes=[rk])
              T.op("dve", lambda e, rd=rd: e.tensor_tensor(out=QA[pb:pb + 64, 4 + hp, :], in0=acc[pb:pb + 64, :],
                                                           in1=rd[db:db + 64, :], op=ALU.mult),
                   reads=[ak, rk], writes=[("QA", 4 + hp, half)])

      def ln_prep(l, tt, YB, ybk, YS, ysk):
          tsl = slice(tt * NT, (tt + 1) * NT)
          xk = [("X", tt, c) for c in range(8)]
          for c in range(8):
              if YS is QA:
                  T.op("dve", lambda e, c=c: e.tensor_copy(out=YB[:, c, :], in_=X[:, c, tsl]),
                       reads=[xk[c]], writes=ybk(c))
              else:
                  T.op("act", lambda e, c=c: e.activation(out=YB[:, c, :], in_=X[:, c, tsl], func=AF.Identity),
                       reads=[xk[c]], writes=ybk(c))
              T.op("act", lambda e, c=c: e.activation(out=YS[:, c, :], in_=X[:, c, tsl], func=AF.Square),
                   reads=[xk[c]], writes=ysk(c))

      def ln_finish(l, tt, G, Bv, YB, ybk, YS, ysk):
          tsl = slice(tt * NT, (tt + 1) * NT)
          xk = [("X", tt, c) for c in range(8)]
          s1, k1 = psg()
          s2, k2 = psg()
          for c in range(8):
              T.op("pe", lambda e, c=c: e.matmul(s1[:, :], lhsT=ONES[:, :], rhs=YB[:, c, :], start=(c == 0),
                                                 stop=(c == 7)), reads=ybk(c) + [("C", "ONES")], writes=[k1])
          for c in range(8):
              T.op("pe", lambda e, c=c: e.matmul(s2[:, :], lhsT=ONES[:, :], rhs=YS[:, c, :], start=(c == 0),
                                                 stop=(c == 7)), reads=ysk(c) + [("C", "ONES")], writes=[k2])
          mean, rstd, nmr = L[0], L[1], L[2]
          T.op("dve", lambda e: e.tensor_scalar_mul(out=mean[:, :], in0=s1[:, :], scalar1=1.0 / D),
               reads=[k1], writes=[("L", 0)])
          T.op("dve", lambda e: e.tensor_tensor(out=nmr[:, :], in0=mean[:, :], in1=mean[:, :], op=ALU.mult),
               reads=[("L", 0)], writes=[("L", 2)])
          T.op("dve", lambda e: e.scalar_tensor_tensor(out=rstd[:, :], in0=s2[:, :], scalar=1.0 / D, in1=nmr[:, :],
                                                       op0=ALU.mult, op1=ALU.subtract),
               reads=[k2, ("L", 2)], writes=[("L", 1)])
          T.op("act", lambda e: e.activation(out=rstd[:, :], in_=rstd[:, :], func=AF.Ln, bias=EPSC[:, 0:1],
                                             scale=1.0), reads=[("L", 1), ("C", "EPSC")], writes=[("L", 1)])
          T.op("act", lambda e: e.activation(out=rstd[:, :], in_=rstd[:, :], func=AF.Exp, scale=-0.5),
               reads=[("L", 1)], writes=[("L", 1)])
          T.op("dve", lambda e: e.scalar_tensor_tensor(out=nmr[:, :], in0=mean[:, :], scalar=-1.0, in1=rstd[:, :],
                                                       op0=ALU.mult, op1=ALU.mult),
               reads=[("L", 0), ("L", 1)], writes=[("L", 2)])
          for c in range(8):
              T.op(LNENG[0], lambda e, c=c: e.tensor_tensor(out=X[:, c, tsl], in0=X[:, c, tsl], in1=rstd[:, :],
                                                         op=ALU.mult), reads=[xk[c], ("L", 1)], writes=[xk[c]])
              T.op(LNENG[0], lambda e, c=c: e.tensor_tensor(out=X[:, c, tsl], in0=X[:, c, tsl], in1=nmr[:, :],
                                                         op=ALU.add), reads=[xk[c], ("L", 2)], writes=[xk[c]])
              T.op(LNENG[0], lambda e, c=c: e.tensor_scalar(out=X[:, c, tsl], in0=X[:, c, tsl],
                                                          scalar1=G[:, l * 8 + c:l * 8 + c + 1],
                                                          scalar2=Bv[:, l * 8 + c:l * 8 + c + 1],
                                                          op0=ALU.mult, op1=ALU.add),
                   reads=[xk[c], CK(G), CK(Bv)], writes=[xk[c]])

      def passB(l, tt, qw):
          tsl = slice(tt * NT, (tt + 1) * NT)
          mo = (l % 2) * 48
          if not go():
              return
          T.lab = "B.q"
          wv, wk = qw
          for hp in range(4):
              ps, pk = proj_fm(wv, wk, hp * 128, lambda kc: HM[:, kc, :], HMall)
              T.op("act", lambda e, hp=hp, ps=ps: e.activation(out=QA[:, hp, :], in_=ps[:, :], func=AF.Identity),
                   reads=[pk], writes=QK(hp))
          if not go():
              return
          T.lab = "B.attn"
          attention(l, tt)
          if not go():
              return
          T.lab = "B.conv+bg"
          for cc in range(4):
              wb = (l * 4 + cc) * 3
              sbuf, sk, _ = nxtS()
              zc = slice(1 + tt * NT, 1 + (tt + 1) * NT)
              zl = slice(tt * NT, (tt + 1) * NT)
              zr = slice(2 + tt * NT, 2 + (tt + 1) * NT)
              zkeys = [("Z", t2, cc) for t2 in (tt - 1, tt, tt + 1) if 0 <= t2 < NTILES] + [("Zpad",)]
              T.op("dve", lambda e, cc=cc, sbuf=sbuf, wb=wb: e.tensor_scalar(
                  out=sbuf[:, :], in0=Z[:, cc, zc], scalar1=CW[:, wb + 1:wb + 2],
                  scalar2=CB[:, l * 4 + cc:l * 4 + cc + 1], op0=ALU.mult, op1=ALU.add),
                  reads=zkeys + [CK(CW), CK(CB)], writes=[sk])
              p1, p1k = nxtPT()
              T.op("dve", lambda e, cc=cc, p1=p1: e.tensor_tensor(out=p1[:, :], in0=Z[:, cc, zl], in1=NF[:, zc],
                                                                  op=ALU.mult),
                   reads=zkeys + [("C", "NF")], writes=[p1k])
              T.op("dve", lambda e, sbuf=sbuf, p1=p1, wb=wb: e.scalar_tensor_tensor(
                  out=sbuf[:, :], in0=p1[:, :], scalar=CW[:, wb:wb + 1], in1=sbuf[:, :], op0=ALU.mult, op1=ALU.add),
                  reads=[p1k, sk, CK(CW)], writes=[sk])
              p2, p2k = nxtPT()
              T.op("dve", lambda e, cc=cc, p2=p2: e.tensor_tensor(out=p2[:, :], in0=Z[:, cc, zr], in1=NF[:, zr],
                                                                  op=ALU.mult),
                   reads=zkeys + [("C", "NF")], writes=[p2k])
              T.op("dve", lambda e, cc=cc, sbuf=sbuf, p2=p2, wb=wb: e.scalar_tensor_tensor(
                  out=QA[:, cc, :], in0=p2[:, :], scalar=CW[:, wb + 2:wb + 3], in1=sbuf[:, :],
                  op0=ALU.mult, op1=ALU.add), reads=[p2k, sk, CK(CW)], writes=QK(cc))
          wv, wk = wloadB(1, wsrc(win_d, l, 0, D, 2048, 512), 8, 512, tt)
          for cc in range(4):
              ps, pk = proj_fm(wv, wk, cc * 128, lambda kc: HM[:, kc, :], HMall)
              T.op("dve", lambda e, cc=cc, ps=ps: e.tensor_tensor(out=QA[:, cc, :], in0=ps[:, :], in1=QA[:, cc, :],
                                                                  op=ALU.mult),
                   reads=[pk] + QK(cc), writes=QK(cc))
          if not go():
              return
          T.lab = "B.gated"
          ATk = [("QA", 4 + i, hf) for i in range(4) for hf in range(2)]
          CTk = [("QA", i, hf) for i in range(4) for hf in range(2)]
          for rnd in range(2):
              gcol0 = 3072 + rnd * 1024
              pw_d = wap_d if rnd == 0 else wcp_d
              src_fn = (lambda kc: QA[:, 4 + kc, :]) if rnd == 0 else (lambda kc: QA[:, kc, :])
              skeys = ATk if rnd == 0 else CTk
              for half in range(2):
                  gv, gk = wloadB(2 + rnd * 2 + half, wsrc(win_d, l, 0, D, gcol0 + half * 512, 512), 8, 512, tt)
                  pv, pvk = wloadB(6 + rnd * 2 + half, wsrc(pw_d, l, 0, 512, half * 512, 512), 4, 512, tt)
                  sgs = []
                  for ol in range(4):
                      ps, pk = proj_fm(gv, gk, ol * 128, lambda kc: HM[:, kc, :], HMall)
                      sg, sgk = nxtPT()
                      T.op("act", lambda e, ps=ps, sg=sg: e.activation(out=sg[:, :], in_=ps[:, :], func=AF.Sigmoid),
                           reads=[pk], writes=[sgk])
                      sgs.append((sg, sgk))
                  for ol in range(4):
                      oc = half * 4 + ol
                      sg, sgk = sgs[ol]
                      ps2, pk2 = proj_fm(pv, pvk, ol * 128, src_fn, skeys, nk=4)
                      if rnd == 0:
                          T.op("dve", lambda e, oc=oc, ps2=ps2, sg=sg: e.tensor_tensor(
                              out=G1[:, oc, :], in0=ps2[:, :], in1=sg[:, :], op=ALU.mult),
                              reads=[pk2, sgk], writes=[("G1", oc)])
                      else:
                          tmp, tk = nxtPT()
                          T.op("dve", lambda e, ps2=ps2, sg=sg, tmp=tmp: e.tensor_tensor(
                              out=tmp[:, :], in0=ps2[:, :], in1=sg[:, :], op=ALU.mult),
                              reads=[pk2, sgk], writes=[tk])
                          T.op("dve", lambda e, oc=oc, tmp=tmp: e.tensor_tensor(
                              out=G1[:, oc, :], in0=G1[:, oc, :], in1=tmp[:, :], op=ALU.add),
                              reads=[tk, ("G1", oc)], writes=[("G1", oc)])
          if not go():
              return
          T.lab = "B.wo"
          G1all = lambda kc: [("G1", kc)]
          for half in range(2):
              wv, wk = wloadB(10 + half, wsrc(wo_d, l, 0, D, half * 512, 512), 8, 512, tt)
              for ol in range(4):
                  oc = half * 4 + ol
                  ps, pk = proj_fm(wv, wk, ol * 128, lambda kc: G1[:, kc, :], G1all)
                  T.op("dve", lambda e, oc=oc, ps=ps: e.scalar_tensor_tensor(
                      out=X[:, oc, tsl], in0=ps[:, :], scalar=MODV[:, mo + 16 + oc:mo + 17 + oc], in1=X[:, oc, tsl],
                      op0=ALU.mult, op1=ALU.add), reads=[pk, ("X", tt, oc), ("MODV", l % 2)],
                      writes=[("X", tt, oc)])

      G1k = lambda c: [("G1", c)]
      HMl = lambda c: [("HM", c)]

      def ln1_prep(l, tt):
          T.lab = "B.ln1"
          ln_prep(l, tt, G1, G1k, QA, QK)

      def ln1_finish(l, tt):
          T.lab = "B.ln1"
          ln_finish(l, tt, L1G, L1B, G1, G1k, QA, QK)

      def passC(l):
          mo = (l % 2) * 48
          if not go():
              return
          T.barrier()
          if l + 1 < DEPTH:
              mod_finish(l + 1)
          T.op("dve", lambda e: e.tensor_tensor(out=GB2[:, :], in0=MODV[:, mo + 40:mo + 48],
                                                in1=B2[:, l * 8:(l + 1) * 8], op=ALU.mult),
               reads=[("MODV", l % 2), CK(B2)], writes=[("C", "GB2")])
          for tt in range(NTILES):
              emit_h(l, tt, 1, lambda c, tt=tt: H2[:, c, tt * NT:(tt + 1) * NT], lambda c, tt=tt: ("H2", tt, c))
              for c in range(8):
                  T.op("act", lambda e, c=c, tt=tt: e.activation(
                      out=X[:, c, tt * NT:(tt + 1) * NT], in_=X[:, c, tt * NT:(tt + 1) * NT], func=AF.Identity,
                      bias=GB2[:, c:c + 1], scale=1.0), reads=[("X", tt, c), ("C", "GB2")], writes=[("X", tt, c)])
          T.lab = "C.mlp"

          def mlp_w1(g, tt, w1v, w1k):
              tsl = slice(tt * NT, (tt + 1) * NT)
              ab = (tt % 2) * 4
              h2k = lambda kc, tt=tt: [("H2", tt, kc)]
              for fc in range(4):
                  ps, pk = proj_fm(w1v, w1k, fc * 128, lambda kc: H2[:, kc, tsl], h2k)
                  r, rk = nxtPT()
                  bi = l * 32 + g * 4 + fc
                  T.op("act", lambda e, ps=ps, r=r, bi=bi: e.activation(
                      out=r[:, :], in_=ps[:, :], func=AF.Relu, bias=B1[:, bi:bi + 1], scale=1.0),
                      reads=[pk, CK(B1)], writes=[rk])
                  T.op("dve", lambda e, r=r, fc=fc, ab=ab: e.tensor_tensor(out=QA[:, ab + fc, :], in0=r[:, :],
                                                                           in1=r[:, :], op=ALU.mult),
                       reads=[rk], writes=QK(ab + fc))
              if g == 7 and tt == NTILES - 1:
                  st["h2_done"] = T.cnt["pe"]

          def mlp_w2(g, tt, w2v, w2k):
              tsl = slice(tt * NT, (tt + 1) * NT)
              ab = (tt % 2) * 4
              ak = [("QA", ab + fc, hf) for fc in range(4) for hf in range(2)]
              for oc in range(8):
                  ps, pk = proj_fm(w2v, w2k, oc * 128, lambda kc: QA[:, ab + kc, :], ak, nk=4)
                  T.op("dve", lambda e, oc=oc, ps=ps, tsl=tsl: e.scalar_tensor_tensor(
                      out=X[:, oc, tsl], in0=ps[:, :], scalar=MODV[:, mo + 40 + oc:mo + 41 + oc],
                      in1=X[:, oc, tsl], op0=ALU.mult, op1=ALU.add),
                      reads=[pk, ("X", tt, oc), ("MODV", l % 2)], writes=[("X", tt, oc)])
              if g == 7:
                  T.lab = "C.ln2"
                  if tt >= 1:
                      ln_finish(l, tt - 1, L2G, L2B, G1, G1k, HM, HMl)
                  ln_prep(l, tt, G1, G1k, HM, HMl)
                  T.lab = "C.mlp"

          for g in range(8):
              w1v, w1k = wload(wsrc(w1_d, l, 0, D, g * 512, 512), 8, 512)
              w2v, w2k = wload(wsrc(w2_d, l, g * 512, 512, 0, D), 4, 1024)
              mlp_w1(g, 0, w1v, w1k)
              for tt in range(NTILES):
                  if tt + 1 < NTILES:
                      mlp_w1(g, tt + 1, w1v, w1k)
                  mlp_w2(g, tt, w2v, w2k)
          T.lab = "C.ln2"
          ln_finish(l, NTILES - 1, L2G, L2B, G1, G1k, HM, HMl)

      emit_mod(0)
      for l in range(DEPTH):
          T.dma("pool", lambda q, l=l: q.dma_start(out=KCX[:, :, :], in_=kctx_d[l]), "ctxk", writes=[("KCX",)])
          T.dma("pool", lambda q, l=l: q.dma_start(out=VCX[:, :, :], in_=vctx_d[l]),
                "ctxv", writes=[("VCX",)])
          if l > 0:
              T._waits("dve", {"pe": st["h2_done"]})
              T.op("dve", lambda e: e.memset(V[:, :, :, 64:128], 1.0), writes=[("V", i) for i in range(16)])
          passA2(l, (0, 1))
          passA2(l, (2, 3))
          emit_h(l, 0, 0, lambda c: HM[:, c, :], HMk)
          qw = wloadB(0, wsrc(win_d, l, 0, D, 0, 512), 8, 512, 0)
          for tt in range(NTILES):
              passB(l, tt, qw)
              ln1_prep(l, tt)
              if tt + 1 < NTILES:
                  T.lab = "B.q"
                  emit_h(l, tt + 1, 0, lambda c: HM[:, c, :], HMk)
                  qw = wloadB(0, wsrc(win_d, l, 0, D, 0, 512), 8, 512, tt + 1)
              ln1_finish(l, tt)
          passC(l)
      for c in range(8):
          T.dma("sp", lambda q, c=c: q.dma_start(out=yT_o[c * 128:(c + 1) * 128, :], in_=X[:, c, :]), "yout",
                reads=[("X", t, c) for t in range(NTILES)])
      for sk in ("yout", "oS0", "oS1", "oS2", "oS3", "oS4"):
          v = T.dma_cnt.get(sk, 0)
          if v and T.target == "sp":
              T.eng.wait_ge(T.sem[sk], v)

    if DRY[0]:
        T0 = Trk(None, None, SEM)
        schedule(T0)
        DRY[0] = T0
        return nc
    with nc.Block() as block:
        @block.tensor
        def _(e):
            schedule(Trk("pe", e, SEM))

        @block.scalar
        def _(e):
            schedule(Trk("act", e, SEM))

        @block.vector
        def _(e):
            schedule(Trk("dve", e, SEM))

        @block.gpsimd
        def _(e):
            schedule(Trk("pool", e, SEM))

        @block.sync
        def _(e):
            schedule(Trk("sp", e, SEM))
    print("sbuf bytes remaining", nc.sbuf_bytes_remaining)
    es.close()
    return nc


def _pc(v, nchunk):
    Ld = v.shape[0]
    return np.ascontiguousarray(v.reshape(Ld, nchunk, 128).transpose(2, 0, 1).reshape(128, Ld * nchunk))


def _core_tables(role, rpb):
    ew = np.zeros((128, 4, 8, 8), np.float32)
    a = (np.arange(128) // 64)[:, None, None, None]
    tt = np.arange(4)[None, :, None, None]
    c = np.arange(8)[None, None, :, None]
    j = np.arange(8)[None, None, None, :]
    kr = 8 * tt - 4 + 2 * c + a
    qr = 8 * tt + j
    inb = (kr >= 0) & (kr < 32)
    if role == "sample":
        sr = np.clip(qr - 4, 0, 24)
        valid = inb & (kr >= sr) & (kr < sr + 8)
    else:
        valid = inb & ((kr // 4) == (qr // 4))
    ew[:] = np.broadcast_to(valid, ew.shape).astype(np.float32)
    nfm = np.zeros((1, NTOK + 2), np.float32)
    t = np.arange(NTOK)
    if role == "sample":
        nfm[0, 1:NTOK + 1] = (t != 0)
    else:
        nfm[0, 1:NTOK + 1] = (t % 256 != 0)
    p = np.arange(128)
    aa = (p // 64)[:, None, None]
    kc = (p % 64)[:, None, None]
    tcol = (np.arange(NTM) + TMIN)[None, :, None]
    qc = np.arange(64)[None, None, :]
    dr = 17 + aa - tcol
    cs = np.clip(qc - 8, 0, 48)
    cm = (kc >= cs) & (kc < cs + 16)
    ok = (dr >= 0) & (dr <= 14) & cm
    dc = np.clip(kc - qc + 15, 0, 30)
    flat_idx = np.where(ok, np.clip(dr, 0, 14) * 31 + dc, 15 * 31)
    flat_idx = np.broadcast_to(flat_idx, (128, NTM, 64)).reshape(128, MW)
    padval = NEG if role == "sample" else 0.0
    rp = np.concatenate([rpb.reshape(DEPTH, H, 15 * 31),
                         np.full((DEPTH, H, 1), padval, np.float32)], axis=2)
    if role == "sample":
        mtab = rp[:, :, flat_idx]
    else:
        mtab = rp[:, :, np.full_like(flat_idx, 15 * 31)]
    return ew.reshape(128, 256), np.ascontiguousarray(mtab.astype(np.float32)), nfm


_NC_CACHE = {}
_HOOK = {}


def kernel(x_prompt, x_sample, cache_k, cache_v, c, c_ctx, w_mod, b_mod, w_in, rpb, conv_w, conv_b,
           w_attn_proj, w_conv_proj, w_o, ln1_g, ln1_b, w1, b1, w2, b2, ln2_g, ln2_b):
    f = lambda a: np.ascontiguousarray(np.asarray(a, dtype=np.float32))
    x_prompt, x_sample, cache_k, cache_v, c, c_ctx = map(f, (x_prompt, x_sample, cache_k, cache_v, c, c_ctx))
    rpb = f(rpb)
    shared = {
        "w_mod": f(w_mod), "bmod_pc": _pc(f(b_mod), 48), "w_in": f(w_in),
        "convw_pc": np.ascontiguousarray(
            f(conv_w).reshape(DEPTH, 3, 4, 128).transpose(3, 0, 2, 1).reshape(128, DEPTH * 12)),
        "convb_pc": _pc(f(conv_b), 4),
        "w_attn_proj": f(w_attn_proj), "w_conv_proj": f(w_conv_proj), "w_o": f(w_o),
        "ln1g_pc": _pc(f(ln1_g), 8), "ln1b_pc": _pc(f(ln1_b), 8), "w1": f(w1), "b1_pc": _pc(f(b1), 32),
        "w2": f(w2), "b2_pc": _pc(f(b2), 8), "ln2g_pc": _pc(f(ln2_g), 8), "ln2b_pc": _pc(f(ln2_b), 8),
    }
    tabs = {r: _core_tables(r, rpb) for r in ("sample", "prompt")}
    in_maps = []
    roles = []
    for core in range(NCORES):
        if core < 2:
            role, b = "sample", core
            xs = x_sample[b]
            cond = c[b]
            kc_ = cache_k[b]
            kctxT = np.ascontiguousarray(kc_.reshape(DEPTH, 4, 2, 256, 64).transpose(0, 2, 4, 1, 3)
                                         .reshape(DEPTH, 128, 4, 256))
            vc_ = cache_v[b]
            vctx = np.ones((DEPTH, 128, 2, 4, 3, 64), np.float32)
            vctx[:, :, :, :, 0:3:2, :] = vc_.reshape(DEPTH, 4, 2, 2, 128, 64).transpose(0, 4, 3, 1, 2, 5)
            vctx = vctx.reshape(DEPTH, 128, 2, 768)
            ctxb = np.zeros((128, 1), np.float32)
        else:
            role = "prompt"
            pc = min(core - 2, 3)
            xs = x_prompt[pc * 8:(pc + 1) * 8].reshape(NTOK, D)
            cond = c_ctx
            kctxT = np.zeros((DEPTH, 128, 4, 256), np.float32)
            vctx = np.zeros((DEPTH, 128, 2, 768), np.float32)
            ctxb = np.full((128, 1), NEG, np.float32)
        ew, mtab, nfm = tabs[role]
        m = dict(shared)
        m.update({"xT": np.ascontiguousarray(xs.T), "cond_pc": np.ascontiguousarray(cond.reshape(8, 128).T),
                  "kctxT": kctxT, "vctx": vctx, "ctxb": ctxb, "ew": ew, "mtab": mtab, "nfm": nfm})
        in_maps.append(m)
        roles.append(role)
    if _HOOK.get("in_maps_only"):
        return in_maps
    if "nc" not in _NC_CACHE:
        _NC_CACHE["nc"] = build_program()
    nc = _NC_CACHE["nc"]
    res = run_bass_kernel_spmd(nc, in_maps, core_ids=list(range(NCORES)))
    R = res.results
    y_sample = np.stack([R[b]["yT"].T for b in range(2)], axis=0).astype(np.float32)
    y_prompt = np.concatenate([R[2 + pc]["yT"].T.reshape(8, 256, D) for pc in range(4)], axis=0).astype(np.float32)
    nk = np.concatenate([R[2 + pc]["kT_out"].reshape(DEPTH, H, HD, 8, 256).transpose(3, 0, 1, 4, 2)
                         for pc in range(4)], axis=0)
    nv = np.concatenate([R[2 + pc]["v_out"].reshape(DEPTH, 8, 256, H, HD).transpose(1, 0, 3, 2, 4)
                         for pc in range(4)], axis=0)
    return (np.ascontiguousarray(y_prompt), np.ascontiguousarray(y_sample),
            np.ascontiguousarray(nk.astype(np.float32)), np.ascontiguousarray(nv.astype(np.float32)))
```
